# Optimizing a Trainium2 kernel written in Bass

```python
import jax, jax.numpy as jnp
from jax import lax
import numpy as np

D_MODEL = 1024
BATCH = 32
SEQ = 2048
DEPTH = 4
DEC_BATCH = 8
DEC_SEQ = 32
PAST_LEN = 1024

CHUNK = 64
HEAD_DIM = 64
MIX_W = D_MODEL
MIX_HEADS = MIX_W // HEAD_DIM
N_MEM = 256
N_HEADS_MEM = 4
N_HEADS_A = MIX_HEADS - N_HEADS_MEM
N_HEADS_B = MIX_HEADS - N_HEADS_MEM
N_A_LAYERS = DEPTH // 2
N_B_LAYERS = DEPTH - N_A_LAYERS
BAND_CHUNKS = 8
WINDOW_B = BAND_CHUNKS * CHUNK
BAND = (BAND_CHUNKS + 1) * CHUNK
REL_CLIP = 128
SB_BLOCK = 128
D_FF = ((8 * D_MODEL // 3 + 255) // 256) * 256
EPS = 1e-6
NEG = -1e30

kernel_name = 'yoco_stickbreak_chunkband_streaming_encoder'


def rmsnorm(x, g):
    xf = x.astype(jnp.float32)
    y = xf * lax.rsqrt(jnp.mean(xf * xf, axis=-1, keepdims=True) + EPS)
    return (y * g.astype(jnp.float32)).astype(x.dtype)


def swiglu(x, w_gu, w_down):
    gate, up = jnp.split(x @ w_gu, 2, axis=-1)
    return (jax.nn.silu(gate) * up) @ w_down


def heads(t, n):
    return t.reshape(t.shape[:-1] + (n, HEAD_DIM))


def _sb_block(q, k, v, q_pos, k_pos):
    z = jnp.einsum('bqhd,bkhd->bhqk', q, k, preferred_element_type=jnp.float32) * (HEAD_DIM ** -0.5)
    vis = k_pos[None, :] < q_pos[:, None]
    log_keep = jnp.where(vis, jax.nn.log_sigmoid(-z), 0.0)
    after = lax.cumsum(log_keep, axis=log_keep.ndim - 1, reverse=True) - log_keep
    w = jnp.where(vis, jnp.exp(jax.nn.log_sigmoid(z) + after), 0.0)
    return jnp.einsum('bhqk,bkhd->bqhd', w.astype(v.dtype), v)


def stick_breaking(q, k, v, q_start):
    b, tq, h, d = q.shape
    blk = min(SB_BLOCK, tq)
    nb = tq // blk
    k_pos = jnp.arange(k.shape[1])
    qb = q.reshape(b, nb, blk, h, d).swapaxes(0, 1)

    def body(args):
        qi, i = args
        q_pos = q_start + i * blk + jnp.arange(blk)
        return _sb_block(qi, k, v, q_pos, k_pos)

    out = lax.map(body, (qb, jnp.arange(nb)))
    return out.swapaxes(0, 1).reshape(b, tq, h, d)


def _band_block(q, k, v, q_pos, k_pos, bias):
    s = jnp.einsum('bqhd,bkhd->bhqk', q, k, preferred_element_type=jnp.float32) * (HEAD_DIM ** -0.5)
    rel = jnp.clip(q_pos[:, None] - k_pos[None, :], -REL_CLIP, REL_CLIP) + REL_CLIP
    s = s + bias.astype(jnp.float32)[:, rel][None]
    qc = q_pos[:, None] // CHUNK
    kc = k_pos[None, :] // CHUNK
    vis = (k_pos[None, :] >= 0) & (kc <= qc) & (kc >= qc - BAND_CHUNKS)
    p = jax.nn.softmax(jnp.where(vis[None, None], s, NEG), axis=-1)
    return jnp.einsum('bhqk,bkhd->bqhd', p.astype(v.dtype), v)


def chunk_band_prompt(q, k, v, bias):
    b, t, h, d = q.shape
    nc = t // CHUNK
    pad = ((0, 0), (WINDOW_B, 0), (0, 0), (0, 0))
    kp = jnp.pad(k, pad)
    vp = jnp.pad(v, pad)
    qc = q.reshape(b, nc, CHUNK, h, d).swapaxes(0, 1)

    def body(args):
        qi, c = args
        start = c * CHUNK
        kb = lax.dynamic_slice_in_dim(kp, start, BAND, axis=1)
        vb = lax.dynamic_slice_in_dim(vp, start, BAND, axis=1)
        q_pos = start + jnp.arange(CHUNK)
        k_pos = start - WINDOW_B + jnp.arange(BAND)
        return _band_block(qi, kb, vb, q_pos, k_pos, bias)

    out = lax.map(body, (qc, jnp.arange(nc)))
    return out.swapaxes(0, 1).reshape(b, t, h, d)


def chunk_band_step(q, k_all, v_all, q_start, bias):
    tq = q.shape[1]
    tk = k_all.shape[1]
    q_pos = q_start + jnp.arange(tq)
    k_pos = q_start + tq - tk + jnp.arange(tk)
    return _band_block(q, k_all, v_all, q_pos, k_pos, bias)


def mem_attend(q, mk, mv):
    s = jnp.einsum('bqhd,bmhd->bhqm', q, mk, preferred_element_type=jnp.float32) * (HEAD_DIM ** -0.5)
    p = jax.nn.softmax(s, axis=-1)
    return jnp.einsum('bhqm,bmhd->bqhd', p.astype(mv.dtype), mv)


def setup_inputs(seed: int = 0) -> dict:
    key = jax.random.key(seed)
    ks = jax.random.split(key, 32)
    f32 = jnp.float32
    a_w = N_HEADS_A * HEAD_DIM
    b_w = N_HEADS_B * HEAD_DIM
    m_w = N_HEADS_MEM * HEAD_DIM
    wc = min(WINDOW_B, PAST_LEN)

    def nrm(k, shape, scale=1.0):
        return jax.random.normal(k, shape, f32) * scale

    def gain(k, shape):
        return 1.0 + 0.05 * jax.random.normal(k, shape, f32)

    dsc = D_MODEL ** -0.5
    return {
        'x_prompt': nrm(ks[0], (BATCH, SEQ, D_MODEL)),
        'x_sample': nrm(ks[1], (DEC_BATCH, DEC_SEQ, D_MODEL)),
        'cache_a_k': nrm(ks[2], (N_A_LAYERS, DEC_BATCH, PAST_LEN, N_HEADS_A, HEAD_DIM)),
        'cache_a_v': nrm(ks[3], (N_A_LAYERS, DEC_BATCH, PAST_LEN, N_HEADS_A, HEAD_DIM)),
        'cache_b_k': nrm(ks[4], (DEC_BATCH, wc, N_HEADS_B, HEAD_DIM)),
        'cache_b_v': nrm(ks[5], (DEC_BATCH, wc, N_HEADS_B, HEAD_DIM)),
        'cache_mem_k': nrm(ks[6], (DEPTH, DEC_BATCH, N_MEM, N_HEADS_MEM, HEAD_DIM)),
        'cache_mem_v': nrm(ks[7], (DEPTH, DEC_BATCH, N_MEM, N_HEADS_MEM, HEAD_DIM)),
        'mem_prompt': nrm(ks[8], (BATCH, N_MEM, D_MODEL)),
        'g_ff1': gain(ks[9], (DEPTH, D_MODEL)),
        'w_ff1_gu': nrm(ks[10], (DEPTH, D_MODEL, 2 * D_FF), dsc),
        'w_ff1_down': nrm(ks[11], (DEPTH, D_FF, D_MODEL), D_FF ** -0.5),
        'g_mix': gain(ks[12], (DEPTH, D_MODEL)),
        'w_in_a': nrm(ks[13], (N_A_LAYERS, D_MODEL, 3 * a_w + m_w), dsc),
        'w_in_b': nrm(ks[14], (N_B_LAYERS, D_MODEL, b_w + m_w), dsc),
        'w_out': nrm(ks[15], (DEPTH, MIX_W, D_MODEL), MIX_W ** -0.5),
        'g_mem': gain(ks[16], (DEPTH, D_MODEL)),
        'w_mem_kv': nrm(ks[17], (DEPTH, D_MODEL, 2 * m_w), dsc),
        'g_kv': gain(ks[18], (D_MODEL,)),
        'w_kv_b': nrm(ks[19], (D_MODEL, 2 * b_w), dsc),
        'rel_bias_b': nrm(ks[20], (N_B_LAYERS, N_HEADS_B, 2 * REL_CLIP + 1), 0.1),
        'g_ff2': gain(ks[21], (DEPTH, D_MODEL)),
        'w_ff2_gu': nrm(ks[22], (DEPTH, D_MODEL, 2 * D_FF), dsc),
        'w_ff2_down': nrm(ks[23], (DEPTH, D_FF, D_MODEL), D_FF ** -0.5),
        'g_final': gain(ks[24], (D_MODEL,)),
    }


def reference(x_prompt, x_sample, cache_a_k, cache_a_v, cache_b_k, cache_b_v, cache_mem_k, cache_mem_v,
              mem_prompt, g_ff1, w_ff1_gu, w_ff1_down, g_mix, w_in_a, w_in_b, w_out, g_mem, w_mem_kv,
              g_kv, w_kv_b, rel_bias_b, g_ff2, w_ff2_gu, w_ff2_down, g_final):
    a_w = N_HEADS_A * HEAD_DIM
    b_w = N_HEADS_B * HEAD_DIM
    m_w = N_HEADS_MEM * HEAD_DIM

    def run(x, mem_k, mem_v, past_a_k, past_a_v, past_b_k, past_b_v):
        bn, t, _ = x.shape
        has_past = past_a_k is not None
        q_start = past_a_k.shape[2] if has_past else 0
        a_k_rows, a_v_rows = [], []
        kb = vb = kb_all = vb_all = None
        for l in range(DEPTH):
            x = x + 0.5 * swiglu(rmsnorm(x, g_ff1[l]), w_ff1_gu[l], w_ff1_down[l])
            h = rmsnorm(x, g_mix[l])
            if l < N_A_LAYERS:
                proj = h @ w_in_a[l]
                q = heads(proj[..., :a_w], N_HEADS_A)
                k = heads(proj[..., a_w:2 * a_w], N_HEADS_A)
                v = heads(proj[..., 2 * a_w:3 * a_w], N_HEADS_A)
                qm = heads(proj[..., 3 * a_w:], N_HEADS_MEM)
                a_k_rows.append(k)
                a_v_rows.append(v)
                if has_past:
                    k = jnp.concatenate([past_a_k[l], k], axis=1)
                    v = jnp.concatenate([past_a_v[l], v], axis=1)
                o_tok = stick_breaking(q, k, v, q_start)
            else:
                j = l - N_A_LAYERS
                proj = h @ w_in_b[j]
                q = heads(proj[..., :b_w], N_HEADS_B)
                qm = heads(proj[..., b_w:], N_HEADS_MEM)
                if has_past:
                    o_tok = chunk_band_step(q, kb_all, vb_all, q_start, rel_bias_b[j])
                else:
                    o_tok = chunk_band_prompt(q, kb, vb, rel_bias_b[j])
            o_mem = mem_attend(qm, mem_k[l], mem_v[l])
            o = jnp.concatenate([o_tok.reshape(bn, t, -1), o_mem.reshape(bn, t, -1)], axis=-1)
            x = x + o @ w_out[l]
            x = x + 0.5 * swiglu(rmsnorm(x, g_ff2[l]), w_ff2_gu[l], w_ff2_down[l])
            if l == N_A_LAYERS - 1:
                kv = rmsnorm(x, g_kv) @ w_kv_b
                kb = heads(kv[..., :b_w], N_HEADS_B)
                vb = heads(kv[..., b_w:], N_HEADS_B)
                if has_past:
                    kb_all = jnp.concatenate([past_b_k, kb], axis=1)
                    vb_all = jnp.concatenate([past_b_v, vb], axis=1)
        return rmsnorm(x, g_final), jnp.stack(a_k_rows), jnp.stack(a_v_rows), kb, vb

    mk_list, mv_list = [], []
    for l in range(DEPTH):
        mkv = rmsnorm(mem_prompt, g_mem[l]) @ w_mem_kv[l]
        mk_list.append(heads(mkv[..., :m_w], N_HEADS_MEM))
        mv_list.append(heads(mkv[..., m_w:], N_HEADS_MEM))
    mem_k_prompt = jnp.stack(mk_list)
    mem_v_prompt = jnp.stack(mv_list)

    y_prompt, a_k_prompt, a_v_prompt, kb_p, vb_p = run(
        x_prompt, mem_k_prompt, mem_v_prompt, None, None, None, None)
    keep = min(WINDOW_B, kb_p.shape[1])
    b_k_prompt = kb_p[:, kb_p.shape[1] - keep:]
    b_v_prompt = vb_p[:, vb_p.shape[1] - keep:]

    y_sample, a_k_sample, a_v_sample, b_k_sample, b_v_sample = run(
        x_sample, cache_mem_k, cache_mem_v, cache_a_k, cache_a_v, cache_b_k, cache_b_v)

    return (y_prompt, y_sample, a_k_prompt, a_v_prompt, b_k_prompt, b_v_prompt, mem_k_prompt, mem_v_prompt,
            a_k_sample, a_v_sample, b_k_sample, b_v_sample)
```

```python
import numpy as np
import concourse.bass as bass
import concourse.mybir as mybir
from concourse.bass_utils import run_bass_kernel_spmd

F32 = mybir.dt.float32
BF16 = mybir.dt.bfloat16
AF = mybir.ActivationFunctionType
ALU = mybir.AluOpType
NEG = -30000.0
EPS = 1e-6
XOFF = 384
GMW = 1408
NMW = 896
WCH = 8192
LOOKAHEAD = 6


class Cfg:
    def __init__(s, D=1024, T=2048, NSEQ=4, DFF=2816, HA=12, HM=4, NMEM=256, PAST=1024, TS=32, L=4, G=2,
                 WB=512, n_cores=8):
        s.D, s.T, s.NSEQ, s.DFF, s.HA, s.HM, s.NMEM, s.PAST, s.TS, s.L, s.G, s.WB = D, T, NSEQ, DFF, HA, HM, NMEM, PAST, TS, L, G, WB
        s.n_cores = n_cores
        s.NCH = D // 128
        s.NF = DFF // 128
        assert s.NF % G == 0
        s.NG = s.NF // G
        s.NPA = HA // 2
        s.NPM = HM // 2
        s.LA = L // 2
        s.LB = L - s.LA
        s.AW = HA * 64
        s.MW = HM * 64
        assert (HA + HM) * 64 == D
        s.WBC = min(WB, PAST)
        s.KEEP = min(WB, T)
        s.KT = max(T, PAST + TS)
        s.NKT = (s.KT + 127) // 128
        s.NGC = (4 * L + 2) * s.NCH

    def gcol(s, name, l=0):
        order = {'ff1': 0, 'mix': 1, 'ff2': 2, 'mem': 3}
        if name == 'kv':
            return 4 * s.L * s.NCH
        if name == 'final':
            return (4 * s.L + 1) * s.NCH
        return (order[name] * s.L + l) * s.NCH


def weight_dir(cfg):
    blocks = []
    NCH, G, D = cfg.NCH, cfg.G, cfg.D
    for l in range(cfg.L):
        for jm in range(cfg.NPM):
            blocks.append((('memkv', l, jm), NCH * 2 * 128))
    for l in range(cfg.L):
        for g in range(cfg.NG):
            blocks.append((('gu', l, 1, g), NCH * 2 * G * 128))
            blocks.append((('dn', l, 1, g), G * D))
        if l < cfg.LA:
            for j in range(cfg.NPA):
                blocks.append((('ina', l, j), NCH * 3 * 128))
                blocks.append((('out', l, j), D))
            for jm in range(cfg.NPM):
                blocks.append((('inam', l, jm), NCH * 128))
                blocks.append((('out', l, cfg.NPA + jm), D))
        else:
            for j in range(cfg.NPA):
                blocks.append((('inb', l - cfg.LA, j), NCH * 128))
                blocks.append((('out', l, j), D))
            for jm in range(cfg.NPM):
                blocks.append((('inbm', l - cfg.LA, jm), NCH * 128))
                blocks.append((('out', l, cfg.NPA + jm), D))
        for g in range(cfg.NG):
            blocks.append((('gu', l, 2, g), NCH * 2 * G * 128))
            blocks.append((('dn', l, 2, g), G * D))
        if l == cfg.LA - 1:
            for j in range(cfg.NPA):
                blocks.append((('kvb', j), NCH * 2 * 128))
    d = {}
    off = 0
    for k, F in blocks:
        d[k] = (off, F)
        off += 128 * F
    chunk = 128 * WCH
    total = ((off + chunk - 1) // chunk) * chunk
    return blocks, d, total


def pack_weights(cfg, w):
    blocks, d, total = weight_dir(cfg)
    out = np.zeros(total, np.float32)
    NCH, G, D, DFF, AW = cfg.NCH, cfg.G, cfg.D, cfg.DFF, cfg.AW

    def kc(m):
        return m.reshape(NCH, 128, -1).transpose(1, 0, 2)

    for key, F in blocks:
        kind = key[0]
        if kind == 'gu':
            _, l, i, g = key
            m = w['w_ff1_gu' if i == 1 else 'w_ff2_gu'][l]
            f0 = g * G * 128
            blk = np.stack([kc(m[:, f0:f0 + G * 128]), kc(m[:, DFF + f0:DFF + f0 + G * 128])], axis=2)
        elif kind == 'dn':
            _, l, i, g = key
            m = w['w_ff1_down' if i == 1 else 'w_ff2_down'][l]
            blk = m[g * G * 128:(g + 1) * G * 128].reshape(G, 128, D).transpose(1, 0, 2)
        elif kind == 'ina':
            _, l, j = key
            m = w['w_in_a'][l]
            blk = np.stack([kc(m[:, s * AW + j * 128: s * AW + (j + 1) * 128]) for s in range(3)], axis=2)
        elif kind == 'inam':
            _, l, jm = key
            m = w['w_in_a'][l]
            blk = kc(m[:, 3 * AW + jm * 128: 3 * AW + (jm + 1) * 128])
        elif kind == 'inb':
            _, lb, j = key
            blk = kc(w['w_in_b'][lb][:, j * 128:(j + 1) * 128])
        elif kind == 'inbm':
            _, lb, jm = key
            blk = kc(w['w_in_b'][lb][:, AW + jm * 128: AW + (jm + 1) * 128])
        elif kind == 'out':
            _, l, j = key
            blk = w['w_out'][l][j * 128:(j + 1) * 128]
        elif kind == 'memkv':
            _, l, jm = key
            m = w['w_mem_kv'][l]
            blk = np.stack([kc(m[:, s * cfg.MW + jm * 128: s * cfg.MW + (jm + 1) * 128]) for s in range(2)], axis=2)
        elif kind == 'kvb':
            _, j = key
            m = w['w_kv_b']
            blk = np.stack([kc(m[:, s * AW + j * 128: s * AW + (j + 1) * 128]) for s in range(2)], axis=2)
        off = d[key][0]
        out[off:off + 128 * F] = np.ascontiguousarray(blk, dtype=np.float32).reshape(-1)
    return out


def pack_gains(cfg, w):
    cols = []
    for name in ('g_ff1', 'g_mix', 'g_ff2', 'g_mem'):
        for l in range(cfg.L):
            cols.append(w[name][l].reshape(cfg.NCH, 128).T)
    cols.append(w['g_kv'].reshape(cfg.NCH, 128).T)
    cols.append(w['g_final'].reshape(cfg.NCH, 128).T)
    return np.ascontiguousarray(np.concatenate(cols, axis=1), dtype=np.float32)


def make_consts(cfg):
    k = np.arange(128)[:, None]
    x = np.arange(NMW)[None, :] - XOFF
    nm = np.where(k >= x, NEG, 0.0).astype(np.float32)
    ident = np.eye(128, dtype=np.float32)
    j = np.arange(128)[:, None]
    s = np.arange(128)[None, :]
    negu = np.where(j >= s, -1.0, 0.0).astype(np.float32)
    negones = -np.ones((128, 128), np.float32)
    onesmean = np.full((128, 128), 1.0 / cfg.D, np.float32)
    return np.ascontiguousarray(np.concatenate([nm, ident, negu, negones, onesmean], axis=1))


def make_gm(cfg, rel_bias_b):
    k = np.arange(128)[:, None]
    x = np.arange(GMW)[None, :] - XOFF
    rel = np.clip(x - k, -128, 128) + 128
    xc = np.floor_divide(x, 64)
    kcn = k // 64
    vis = (kcn <= xc) & (xc <= kcn + 8)
    g = rel_bias_b[:, :, rel]
    return np.ascontiguousarray(np.where(vis[None, None], g, np.float32(NEG)), dtype=np.float32)


class Op:
    __slots__ = ('eng', 'fn', 'deps', 'dma', 'sig', 'sem', 'val', 'slot_prev')

    def __init__(s, eng, fn, dma):
        s.eng, s.fn, s.dma = eng, fn, dma
        s.deps = []
        s.sig = dma
        s.sem = None
        s.val = 0
        s.slot_prev = None


ENGS = ('pe', 'act', 'dve', 'pool', 'sp')


class Prog:
    def __init__(s):
        s.ops = []
        s.lw = {}
        s.rd = {}

    def add(s, eng, fn, reads=(), writes=(), dma=False):
        op = Op(eng, fn, dma)
        oid = len(s.ops)
        deps = set()
        for r in reads:
            w = s.lw.get(r)
            if w is not None:
                deps.add(w)
        for wt in writes:
            w = s.lw.get(wt)
            if w is not None:
                wo = s.ops[w]
                if wo.dma or dma or wo.eng != eng:
                    deps.add(w)
            rdd = s.rd.get(wt)
            if rdd:
                for (reng, rdma, _), rid in rdd.items():
                    if rdma or dma or reng != eng:
                        deps.add(rid)
        final = []
        for dd in deps:
            do = s.ops[dd]
            if (not do.dma) and (not dma) and do.eng == eng and eng == 'pe':
                continue
            final.append(dd)
            do.sig = True
        op.deps = final
        s.ops.append(op)
        for r in reads:
            dct = s.rd.setdefault(r, {})
            if dma:
                dct[(eng, True, oid)] = oid
            else:
                dct[(eng, False, 0)] = oid
        for wt in writes:
            s.lw[wt] = oid
            s.rd[wt] = {}
        return oid

    def emit(s, nc, n_dma_sems=None):
        n_dma_sems = n_dma_sems or {'sp': 24, 'pool': 48}
        csem = {e: nc.alloc_semaphore('s_' + e) for e in ('pe', 'act', 'dve', 'pool')}
        dsem = {q: [nc.alloc_semaphore('d_%s%d' % (q, i)) for i in range(n)] for q, n in n_dma_sems.items()}
        cnt = {e: 0 for e in csem}
        dcount = {q: 0 for q in dsem}
        slot_cnt = {q: [0] * len(dsem[q]) for q in dsem}
        for op in s.ops:
            if op.dma:
                q = op.eng
                i = dcount[q] % len(dsem[q])
                dcount[q] += 1
                op.slot_prev = slot_cnt[q][i]
                slot_cnt[q][i] += 16
                op.sem = dsem[q][i]
                op.val = slot_cnt[q][i]
            elif op.sig:
                cnt[op.eng] += 1
                op.sem = csem[op.eng]
                op.val = cnt[op.eng]
        per = {e: [] for e in ENGS}
        for op in s.ops:
            per[op.eng].append(op)
        ops = s.ops
        final_waits = []
        for q in dsem:
            for i, sm in enumerate(dsem[q]):
                if slot_cnt[q][i] > 0:
                    final_waits.append((sm, slot_cnt[q][i]))
        stats = {e: [len(per[e]), 0] for e in ENGS}

        def run(ename, e):
            waited = {}
            nw = 0
            for op in per[ename]:
                waits = {}
                for dd in op.deps:
                    do = ops[dd]
                    k = id(do.sem)
                    if k not in waits or waits[k][1] < do.val:
                        waits[k] = (do.sem, do.val)
                if op.dma and op.slot_prev:
                    k = id(op.sem)
                    if k not in waits or waits[k][1] < op.slot_prev:
                        waits[k] = (op.sem, op.slot_prev)
                for k, (sm, v) in waits.items():
                    if waited.get(k, 0) < v:
                        e.wait_ge(sm, v)
                        waited[k] = v
                        nw += 1
                ins = op.fn(e)
                if op.sig:
                    ins.then_inc(op.sem, 16 if op.dma else 1)
            if ename == 'sp':
                for sm, v in final_waits:
                    if waited.get(id(sm), 0) < v:
                        e.wait_ge(sm, v)
            stats[ename][1] = nw

        with nc.Block() as block:
            @block.tensor
            def _(e):
                run('pe', e)

            @block.scalar
            def _(e):
                run('act', e)

            @block.vector
            def _(e):
                run('dve', e)

            @block.gpsimd
            def _(e):
                run('pool', e)

            @block.sync
            def _(e):
                run('sp', e)
        return stats


class Rot:
    def __init__(s, items):
        s.items = list(items)
        s.i = 0

    def get(s):
        v = s.items[s.i % len(s.items)]
        s.i += 1
        return v


class Builder:
    def __init__(s, cfg):
        s.cfg = cfg
        s.P = Prog()
        s.nc = bass.Bass("TRN2", target_bir_lowering=False)
        s.blocks, s.wdir, s.wtotal = weight_dir(cfg)
        s._decl()

    def _decl(s):
        nc, c = s.nc, s.cfg
        di = lambda n, sh: nc.dram_tensor(n, sh, F32, kind="ExternalInput").ap()
        do = lambda n, sh: nc.dram_tensor(n, sh, F32, kind="ExternalOutput").ap()
        s.xT_p = di("xT_p", [c.NSEQ, c.D, c.T])
        s.memT_p = di("memT_p", [c.NSEQ, c.D, c.NMEM])
        s.xT_s = di("xT_s", [c.D, c.TS])
        s.cakT = di("cakT", [c.LA, c.AW, c.PAST])
        s.cav = di("cav", [c.LA, c.PAST, c.AW])
        s.cbkT = di("cbkT", [c.AW, c.WBC])
        s.cbv = di("cbv", [c.WBC, c.AW])
        s.cmkT = di("cmkT", [c.L, c.MW, c.NMEM])
        s.cmv = di("cmv", [c.L, c.NMEM, c.MW])
        s.gains_d = di("gains", [128, c.NGC])
        s.consts_d = di("consts", [128, GMW])
        s.gm_d = di("gm", [c.LB, c.HA, 128, GMW])
        s.wpack = di("wpack", [s.wtotal])
        s.yT_p = do("yT_p", [c.NSEQ, c.D, c.T])
        s.yT_s = do("yT_s", [c.D, c.TS])
        s.akT_p = do("akT_p", [c.LA, c.NSEQ, c.AW, c.T])
        s.av_p = do("av_p", [c.LA, c.NSEQ, c.NPA, c.T, 128])
        s.bkT_p = do("bkT_p", [c.NSEQ, c.AW, c.KEEP])
        s.bv_p = do("bv_p", [c.NSEQ, c.NPA, c.KEEP, 128])
        s.mkT_p = do("mkT_p", [c.L, c.NSEQ, c.MW, c.NMEM])
        s.mv_p = do("mv_p", [c.L, c.NSEQ, c.NPM, c.NMEM, 128])
        s.akT_s = do("akT_s", [c.LA, c.AW, c.TS])
        s.av_s = do("av_s", [c.LA, c.NPA, c.TS, 128])
        s.bkT_s = do("bkT_s", [c.AW, c.TS])
        s.bv_s = do("bv_s", [c.NPA, c.TS, 128])
        ds = lambda n, sh: nc.dram_tensor(n, sh, BF16, kind="Internal").ap()
        s.wsc = ds("wsc", [s.wtotal])
        s.kb_sc = ds("kb_sc", [c.NSEQ + 1, c.NPA, 128, c.T])
        s.vb_sc = ds("vb_sc", [c.NSEQ + 1, c.NPA, c.T, 256])
        s.mk_sc = ds("mk_sc", [c.L, c.NPM, 128, c.NMEM])
        s.mv_sc = ds("mv_sc", [c.L, c.NPM, c.NMEM, 256])
        sb = lambda n, sh, dt: nc.alloc_sbuf_tensor(n, sh, dt)
        s.xT = sb("xT", [128, c.NCH, c.T], F32)
        s.hT = sb("hT", [128, c.NCH, c.T], BF16)
        s.wgu = sb("wgu", [128, 2, c.NCH * 2 * c.G * 128], BF16)
        s.wd = sb("wd", [128, 2, c.G * c.D], BF16)
        s.win = sb("win", [128, 2, c.NCH * 3 * 128], BF16)
        s.wout = sb("wout", [128, 2, c.D], BF16)
        s.qb = sb("qb", [128, c.T], BF16)
        s.kb = sb("kb", [128, c.KT], BF16)
        s.vb = sb("vb", [128, c.NKT, 256], BF16)
        s.oT = sb("oT", [128, 2, c.T], BF16)
        s.NFP, s.NBP = 6, 12
        s.fp = sb("fp", [128, s.NFP, 512], F32)
        s.bp = sb("bp", [128, s.NBP, 512], BF16)
        s.rs = sb("rs", [128, 2, 512], F32)
        s.rsr = Rot([0, 1])
        s.gm = sb("gmb", [128, 2, GMW], BF16)
        s.cst = sb("cst", [128, GMW], BF16)
        s.gains = sb("gains_sb", [128, c.NGC], F32)
        s.mk = sb("mk", [128, c.NPM, c.NMEM], BF16)
        s.mv = sb("mv", [128, c.NMEM // 128, c.NPM * 256], BF16)
        s.pd = [nc.alloc_psum_tensor("pd%d" % i, [128, 2, 512], F32) for i in range(4)]
        s.ps = [s.pd[b // 2][:, b % 2, :] for b in range(8)]
        s.lac = sb("lac", [128, 2, 2, 512], BF16)
        s.lacr = Rot([0, 1])
        s.zdr = Rot([0, 1, 3])
        s.fdr = Rot(range(s.NFP // 2))
        s.bdr = Rot(range(s.NBP // 2))
        s.opq = []
        s.opr = Rot([0, 1, 2, 3, 6, 7])
        s.fpr = Rot(range(s.NFP))
        s.bpr = Rot(range(s.NBP))
        s.zr = Rot([0, 1, 2, 3])
        s.orr = Rot([4, 5])
        s.mr = Rot([6, 7])
        s.gr = Rot([0, 1])
        s.ur = Rot([2, 3])
        s.dr = Rot([4, 5, 6, 7])
        s.wgur = Rot([0, 1])
        s.winr = Rot([0, 1])
        s.woutr = Rot([0, 1])
        s.otr = Rot([0, 1])
        s.gmr = Rot([0, 1])
        s.nm = s.cst[:, 0:NMW]
        s.ident = s.cst[:, NMW:NMW + 128]
        s.negu = s.cst[:, NMW + 128:NMW + 256]
        s.negones = s.cst[:, NMW + 256:NMW + 384]
        s.onesmean = s.cst[:, NMW + 384:NMW + 512]

    def mm(s, out, lhsT, rhs, start, stop, reads, writes):
        s.P.add('pe', lambda e: e.matmul(out, lhsT=lhsT, rhs=rhs, start=start, stop=stop, skip_group_check=True),
                reads, writes)

    def act(s, out, in_, func, reads, writes, scale=1.0, bias=0.0):
        s.P.add('act', lambda e: e.activation(out=out, in_=in_, func=func, bias=bias, scale=scale), reads, writes)

    def tt(s, eng, out, in0, in1, op, reads, writes):
        s.P.add(eng, lambda e: e.tensor_tensor(out=out, in0=in0, in1=in1, op=op), reads, writes)

    def ts(s, eng, out, in0, sc, op, reads, writes):
        s.P.add(eng, lambda e: e.tensor_scalar(out=out, in0=in0, scalar1=sc, scalar2=None, op0=op), reads, writes)

    def stt(s, eng, out, in0, sc, in1, op0, op1, reads, writes):
        s.P.add(eng, lambda e: e.scalar_tensor_tensor(out=out, in0=in0, scalar=sc, in1=in1, op0=op0, op1=op1),
                reads, writes)

    def cp(s, eng, out, in_, reads, writes):
        s.P.add(eng, lambda e: e.tensor_copy(out=out, in_=in_), reads, writes)

    def dma(s, q, out, in_, reads, writes):
        s.P.add(q, lambda e: e.dma_start(out=out, in_=in_), reads, writes, dma=True)

    def wblock(s, key):
        off, F = s.wdir[key]
        return s.wsc[off:off + 128 * F].rearrange("(p f) -> p f", p=128), F, off

    def wtok(s, off, F):
        ch = 128 * WCH
        return [('wc', k) for k in range(off // ch, (off + 128 * F - 1) // ch + 1)]

    def ensure_chunks(s, upto):
        nchunks = s.wtotal // (128 * WCH)
        src = s.wpack.rearrange("(k p f) -> k p f", p=128, f=WCH)
        dst = s.wsc.rearrange("(k p f) -> k p f", p=128, f=WCH)
        while s.chunk_issued <= min(upto, nchunks - 1):
            k = s.chunk_issued
            s.dma('pool', dst[k], src[k], [], [('wc', k)])
            s.chunk_issued += 1

    def load_w(s, key, dst, dtok):
        ap, F, off = s.wblock(key)
        toks = s.wtok(off, F)
        s.ensure_chunks(toks[-1][1] + LOOKAHEAD)
        s.dma('sp', dst, ap, toks, [dtok])

    def prologue(s):
        c = s.cfg
        s.dma('pool', s.cst[:], s.consts_d, [], [('cst',)])
        s.dma('sp', s.gains[:], s.gains_d, [], [('gains',)])
        s.chunk_issued = 0
        s.ensure_chunks(LOOKAHEAD)
        s.P.add('pool', lambda e: e.memset(s.vb[:, :, 64:192], 1.0), [], [('v', i) for i in range((c.NKT + 3) // 4)])
        for jm in range(c.NPM):
            s.P.add('pool', lambda e, jm=jm: e.memset(s.mv[:, :, jm * 256 + 64: jm * 256 + 192], 1.0), [], [('mv',)])

    def norm_stats(s, tile, src_tok='x'):
        c = s.cfg
        (ti, t0, n) = tile
        pb = s.mr.get()
        for ch in range(c.NCH):
            b = s.bpr.get()
            s.act(s.bp[:, b, 0:n], s.xT[:, ch, t0:t0 + n], AF.Square, [(src_tok, ch, ti)], [('bp', b)])
            s.mm(s.ps[pb][:, 0:n], s.onesmean, s.bp[:, b, 0:n], ch == 0, ch == c.NCH - 1,
                 [('bp', b), ('cst',)], [('ps', pb)])
        f1 = s.fpr.get()
        s.act(s.fp[:, f1, 0:n], s.ps[pb][:, 0:n], AF.Ln, [('ps', pb)], [('fp', f1)], bias=EPS)
        r = s.rsr.get()
        s.act(s.rs[:, r, 0:n], s.fp[:, f1, 0:n], AF.Exp, [('fp', f1)], [('rs', r)], scale=-0.5)
        return r

    def norm_apply(s, tile, r, gc, htok='h'):
        c = s.cfg
        (ti, t0, n) = tile
        for ch in range(c.NCH):
            s.stt('dve', s.hT[:, ch, t0:t0 + n], s.xT[:, ch, t0:t0 + n], s.gains[:, gc + ch:gc + ch + 1],
                  s.rs[:, r, 0:n], ALU.mult, ALU.mult,
                  [('x', ch, ti), ('rs', r), ('gains',)], [(htok, ch, ti)])

    def norm(s, tiles, gc):
        for tile in tiles:
            r = s.norm_stats(tile)
            s.norm_apply(tile, r, gc)

    def resid(s, dc, ti, t0, n, pd, scale, path):
        if path == 0:
            if scale == 1.0:
                s.tt('dve', s.xT[:, dc, t0:t0 + n], s.ps[pd][:, 0:n], s.xT[:, dc, t0:t0 + n], ALU.add,
                     [('ps', pd), ('x', dc, ti)], [('x', dc, ti)])
            else:
                s.stt('dve', s.xT[:, dc, t0:t0 + n], s.ps[pd][:, 0:n], scale, s.xT[:, dc, t0:t0 + n], ALU.mult,
                      ALU.add, [('ps', pd), ('x', dc, ti)], [('x', dc, ti)])
        else:
            f = s.fpr.get()
            s.act(s.fp[:, f, 0:n], s.ps[pd][:, 0:n], AF.Copy, [('ps', pd)], [('fp', f)], scale=scale)
            s.tt('pool', s.xT[:, dc, t0:t0 + n], s.xT[:, dc, t0:t0 + n], s.fp[:, f, 0:n], ALU.add,
                 [('fp', f), ('x', dc, ti)], [('x', dc, ti)])

    def ffn(s, tiles, l, i):
        c = s.cfg
        G, NCH = c.G, c.NCH
        gc = c.gcol('ff1' if i == 1 else 'ff2', l)
        nt = len(tiles)

        def do_norm(k):
            if k < nt:
                r = s.norm_stats(tiles[k])
                s.norm_apply(tiles[k], r, gc)

        def load(g):
            b = s.wgur.get()
            s.load_w(('gu', l, i, g), s.wgu[:, b, :], ('wgu', b))
            s.load_w(('dn', l, i, g), s.wd[:, b, :], ('wd', b))
            return b

        def GU(b, tile):
            (ti, t0, n) = tile
            acts = []
            for fi in range(G):
                pg, pu = s.gr.get(), s.ur.get()
                for ch in range(NCH):
                    base = (ch * 2 + 0) * G * 128 + fi * 128
                    s.mm(s.ps[pg][:, 0:n], s.wgu[:, b, base:base + 128], s.hT[:, ch, t0:t0 + n], ch == 0,
                         ch == NCH - 1, [('wgu', b), ('h', ch, ti)], [('ps', pg)])
                for ch in range(NCH):
                    base = (ch * 2 + 1) * G * 128 + fi * 128
                    s.mm(s.ps[pu][:, 0:n], s.wgu[:, b, base:base + 128], s.hT[:, ch, t0:t0 + n], ch == 0,
                         ch == NCH - 1, [('wgu', b), ('h', ch, ti)], [('ps', pu)])
                f = s.fpr.get()
                s.act(s.fp[:, f, 0:n], s.ps[pg][:, 0:n], AF.Silu, [('ps', pg)], [('fp', f)])
                a = s.bpr.get()
                s.tt('dve', s.bp[:, a, 0:n], s.ps[pu][:, 0:n], s.fp[:, f, 0:n], ALU.mult,
                     [('ps', pu), ('fp', f)], [('bp', a)])
                acts.append(a)
            return acts

        def DOWN(b, tile, acts):
            (ti, t0, n) = tile
            for dc in range(NCH):
                pd = s.dr.get()
                for fi in range(G):
                    s.mm(s.ps[pd][:, 0:n], s.wd[:, b, fi * c.D + dc * 128: fi * c.D + (dc + 1) * 128],
                         s.bp[:, acts[fi], 0:n], fi == 0, fi == G - 1, [('wd', b), ('bp', acts[fi])], [('ps', pd)])
                s.resid(dc, ti, t0, n, pd, 0.5, dc % 2)

        do_norm(0)
        do_norm(1)
        bufs = {0: load(0)}
        steps = [(g, k) for g in range(c.NG) for k in range(nt)]
        prev = None
        for (g, k) in steps:
            acts = GU(bufs[g], tiles[k])
            if prev is not None:
                DOWN(*prev)
            prev = (bufs[g], tiles[k], acts)
            if k == 0 and g + 1 < c.NG:
                bufs[g + 1] = load(g + 1)
            if g == 0:
                do_norm(k + 2)
        DOWN(*prev)

    def proj_fm(s, tiles, wbuf, colf, evac):
        c = s.cfg
        for (ti, t0, n) in tiles:
            pb = s.mr.get()
            for ch in range(c.NCH):
                cb = colf(ch)
                s.mm(s.ps[pb][:, 0:n], s.win[:, wbuf, cb:cb + 128], s.hT[:, ch, t0:t0 + n], ch == 0, ch == c.NCH - 1,
                     [('win', wbuf), ('h', ch, ti)], [('ps', pb)])
            evac(ti, t0, n, pb)

    def proj_tm(s, T, wbuf, colf, evac):
        c = s.cfg
        ntt = (T + 127) // 128
        for u0 in range(0, ntt, 4):
            nu = min(4, ntt - u0)
            pb = s.mr.get()
            rows = min(128, T - u0 * 128)
            for uu in range(nu):
                tk0 = (u0 + uu) * 128
                r = min(128, T - tk0)
                for ch in range(c.NCH):
                    cb = colf(ch)
                    s.mm(s.ps[pb][0:r, uu * 128:(uu + 1) * 128], s.hT[:, ch, tk0:tk0 + r], s.win[:, wbuf, cb:cb + 128],
                         ch == 0, ch == c.NCH - 1, [('win', wbuf), ('h', ch, tk0 // 512)], [('ps', pb)])
            evac(u0, nu, rows, pb)

    def sb_attn(s, groups, ob):
        OD = 2
        items = []
        for g in groups:
            nb = len(g['blocks'])
            g['lac'] = s.lacr.get()
            for bi, (k0, ks, moff, qs) in enumerate(g['blocks']):
                items.append(dict(g=g, k0=k0, ks=ks, moff=moff, qs=qs, first=(bi == 0), last=(bi == nb - 1)))
        n_it = len(items)
        pdt = lambda d: [('ps', 2 * d), ('ps', 2 * d + 1)]
        bpt = lambda d: [('bp', 2 * d), ('bp', 2 * d + 1)]
        fpt = lambda d: [('fp', 2 * d), ('fp', 2 * d + 1)]

        def A(it):
            g = it['g']
            q0, n, ks, k0, qs = g['q0'], g['n'], it['ks'], it['k0'], it['qs']
            zd = s.zdr.get()
            it['zd'] = zd
            if it['first']:
                lc = g['lac']
                s.P.add('pool', lambda e: e.memset(s.lac[:, lc, :, 0:n], 0.0), [], [('lac', lc)])
            for hh in range(2):
                s.mm(s.pd[zd][0:ks, hh, qs:n], s.kb[64 * hh:64 * hh + 64, k0:k0 + ks],
                     s.qb[64 * hh:64 * hh + 64, q0 + qs:q0 + n], True, False,
                     [('k', k0 // 512), ('q', g['qtok'])], [('ps', 2 * zd + hh)])
            if it['moff'] is not None:
                mo = it['moff']
                for hh in range(2):
                    s.mm(s.pd[zd][0:ks, hh, qs:n], s.ident[0:ks, 0:ks], s.nm[0:ks, mo + qs:mo + n], False, False,
                         [('cst',)], [('ps', 2 * zd + hh)])
            e = s.fdr.get()
            s.act(s.fp[0:ks, 2 * e:2 * e + 2, qs:n], s.pd[zd][0:ks, :, qs:n], AF.Exp, pdt(zd), fpt(e))
            sp = s.bdr.get()
            it['sp'] = sp
            s.act(s.bp[0:ks, 2 * sp:2 * sp + 2, qs:n], s.fp[0:ks, 2 * e:2 * e + 2, qs:n], AF.Ln, fpt(e), bpt(sp),
                  bias=1.0)

        def B(it):
            g = it['g']
            n, ks, zd, sp, qs = g['n'], it['ks'], it['zd'], it['sp'], it['qs']
            first = it['first']
            lc = g['lac']
            for hh in range(2):
                s.mm(s.pd[zd][0:ks, hh, qs:n], s.negu[0:ks, 0:ks], s.bp[0:ks, 2 * sp + hh, qs:n], False, first,
                     [('bp', 2 * sp + hh), ('cst',)], [('ps', 2 * zd + hh)])
            if not first:
                for hh in range(2):
                    s.mm(s.pd[zd][0:ks, hh, qs:n], s.negones[:, 0:ks], s.lac[:, lc, hh, qs:n], False, True,
                         [('lac', lc), ('cst',)], [('ps', 2 * zd + hh)])
            if not it['last']:
                s.tt('pool', s.lac[0:ks, lc, :, qs:n], s.lac[0:ks, lc, :, qs:n], s.bp[0:ks, 2 * sp:2 * sp + 2, qs:n],
                     ALU.add, [('lac', lc)] + bpt(sp), [('lac', lc)])
            w = s.bdr.get()
            it['w'] = w
            s.act(s.bp[0:ks, 2 * w:2 * w + 2, qs:n], s.pd[zd][0:ks, :, qs:n], AF.Exp, pdt(zd), bpt(w))

        def C(it):
            g = it['g']
            q0, n, ks, k0, w, qs = g['q0'], g['n'], it['ks'], it['k0'], it['w'], it['qs']
            kt = k0 // 128
            for hh in range(2):
                s.mm(s.pd[OD][:, hh, qs:n], s.vb[0:ks, kt, hh * 128:(hh + 1) * 128], s.bp[0:ks, 2 * w + hh, qs:n],
                     it['first'], it['last'], [('v', kt // 4), ('bp', 2 * w + hh)], [('ps', 2 * OD + hh)])
            if it['last']:
                for hh in range(2):
                    s.cp('dve', s.oT[64 * hh:64 * hh + 64, ob, q0:q0 + n], s.pd[OD][64 * hh:64 * hh + 64, hh, 0:n],
                         [('ps', 2 * OD + hh)], [('o', ob, g['qtok'])])

        for st in range(n_it + 2):
            if st < n_it:
                A(items[st])
            if 0 <= st - 1 < n_it:
                B(items[st - 1])
            if 0 <= st - 2 < n_it:
                C(items[st - 2])

    def sm_attn(s, groups, ob, ksrc, vsrc, gmbuf=None):
        OD = 2
        items = []
        for g in groups:
            nb = len(g['blocks'])
            for bi, (k0, ks, goff) in enumerate(g['blocks']):
                items.append(dict(g=g, k0=k0, ks=ks, goff=goff, first=(bi == 0), last=(bi == nb - 1)))
        n_it = len(items)
        pdt = lambda d: [('ps', 2 * d), ('ps', 2 * d + 1)]
        bpt = lambda d: [('bp', 2 * d), ('bp', 2 * d + 1)]

        def A(it):
            g = it['g']
            q0, n, ks, k0 = g['q0'], g['n'], it['ks'], it['k0']
            zd = s.zdr.get()
            it['zd'] = zd
            has_g = it['goff'] is not None
            for hh in range(2):
                kap, ktok = ksrc(hh, k0, ks)
                s.mm(s.pd[zd][0:ks, hh, 0:n], kap, s.qb[64 * hh:64 * hh + 64, q0:q0 + n], True, not has_g,
                     [ktok, ('q', g['qtok'])], [('ps', 2 * zd + hh)])
            if has_g:
                go = it['goff']
                for hh in range(2):
                    s.mm(s.pd[zd][0:ks, hh, 0:n], s.ident[0:ks, 0:ks], s.gm[0:ks, gmbuf[hh], go:go + n], False, True,
                         [('cst',), ('gm', gmbuf[hh])], [('ps', 2 * zd + hh)])
            w = s.bdr.get()
            it['w'] = w
            s.act(s.bp[0:ks, 2 * w:2 * w + 2, 0:n], s.pd[zd][0:ks, :, 0:n], AF.Exp, pdt(zd), bpt(w))

        def C(it):
            g = it['g']
            q0, n, ks, k0, w = g['q0'], g['n'], it['ks'], it['k0'], it['w']
            for hh in range(2):
                vap, vtok = vsrc(hh, k0, ks)
                s.mm(s.pd[OD][:, hh, 0:n], vap, s.bp[0:ks, 2 * w + hh, 0:n], it['first'], it['last'],
                     [vtok, ('bp', 2 * w + hh)], [('ps', 2 * OD + hh)])
            if it['last']:
                for hh in range(2):
                    f = s.fpr.get()
                    dlo = 64 * (1 - hh)
                    s.act(s.fp[64 * hh:64 * hh + 64, f, 0:n], s.pd[OD][dlo:dlo + 64, hh, 0:n], AF.Ln,
                          [('ps', 2 * OD + hh)], [('fp', f)])
                    s.act(s.fp[64 * hh:64 * hh + 64, f, 0:n], s.fp[64 * hh:64 * hh + 64, f, 0:n], AF.Exp,
                          [('fp', f)], [('fp', f)], scale=-1.0)
                    s.tt('dve', s.oT[64 * hh:64 * hh + 64, ob, q0:q0 + n], s.pd[OD][64 * hh:64 * hh + 64, hh, 0:n],
                         s.fp[64 * hh:64 * hh + 64, f, 0:n], ALU.mult, [('ps', 2 * OD + hh), ('fp', f)],
                         [('o', ob, g['qtok'])])

        for st in range(n_it + 2):
            if st < n_it:
                A(items[st])
            if 0 <= st - 2 < n_it:
                C(items[st - 2])

    def out_proj(s, tiles, l, j, ob, flush=None):
        c = s.cfg
        wb = s.woutr.get()
        s.load_w(('out', l, j), s.wout[:, wb, :], ('wout', wb))
        s.opq.append((wb, ob))
        if len(s.opq) < 2 and not flush:
            return
        q = s.opq
        s.opq = []
        s.opr = Rot([0, 1, 2, 3, 6, 7])
        k = 0
        for (ti, t0, n) in tiles:
            for dc in range(c.NCH):
                pd = s.opr.get()
                for qi, (wbb, obb) in enumerate(q):
                    s.mm(s.ps[pd][:, 0:n], s.wout[:, wbb, dc * 128:(dc + 1) * 128], s.oT[:, obb, t0:t0 + n],
                         qi == 0, qi == len(q) - 1, [('wout', wbb), ('o', obb, ti)], [('ps', pd)])
                s.resid(dc, ti, t0, n, pd, 1.0, 1 if k % 3 == 2 else 0)
                k += 1

    def mem_pair(s, tiles, l, jm, wkey, sample):
        c = s.cfg
        wb = s.winr.get()
        s.load_w(wkey, s.win[:, wb, 0:c.NCH * 128], ('win', wb))

        def evq(ti, t0, n, pb):
            s.ts('dve', s.qb[:, t0:t0 + n], s.ps[pb][:, 0:n], 0.125, ALU.mult, [('ps', pb)], [('q', ti)])

        s.proj_fm(tiles, wb, lambda ch: ch * 128, evq)
        ob = s.otr.get()
        groups = []
        for (ti, t0, n) in tiles:
            groups.append(dict(q0=t0, n=n, qtok=ti, blocks=[(k0, 128, None) for k0 in range(0, c.NMEM, 128)]))
        ksrc = lambda hh, k0, ks: (s.mk[64 * hh:64 * hh + 64, jm, k0:k0 + ks], ('mk',))
        vsrc = lambda hh, k0, ks: (s.mv[0:ks, k0 // 128, jm * 256 + hh * 128: jm * 256 + (hh + 1) * 128], ('mv',))
        s.sm_attn(groups, ob, ksrc, vsrc)
        s.out_proj(tiles, l, c.NPA + jm, ob, flush=(jm == c.NPM - 1))

    def load_mem_kv(s, l, sample):
        c = s.cfg
        if sample:
            for jm in range(c.NPM):
                s.dma('pool', s.mk[:, jm, :], s.cmkT[l, jm * 128:(jm + 1) * 128, :], [], [('mk',)])
                for hh in range(2):
                    col = jm * 256 + (0 if hh == 0 else 192)
                    src = s.cmv[l, :, jm * 128 + hh * 64: jm * 128 + hh * 64 + 64].rearrange("(u p) d -> p u d", p=128)
                    s.dma('pool', s.mv[:, :, col:col + 64], src, [], [('mv',)])
        else:
            for jm in range(c.NPM):
                s.dma('sp', s.mk[:, jm, :], s.mk_sc[l, jm], [('mksc', l, jm)], [('mk',)])
                s.dma('sp', s.mv[:, :, jm * 256:(jm + 1) * 256], s.mv_sc[l, jm].rearrange("(u p) f -> p u f", p=128),
                      [('mvsc', l, jm)], [('mv',)])

    def mem_prep(s, sq):
        c = s.cfg
        NM = c.NMEM
        s.dma('sp', s.xT[:, :, 0:NM], s.memT_p[sq].rearrange("(c p) t -> p c t", p=128), [],
              [('x', ch, 0) for ch in range(c.NCH)])
        tiles = [(0, 0, NM)]
        r = s.norm_stats(tiles[0])
        for l in range(c.L):
            s.norm_apply(tiles[0], r, c.gcol('mem', l))
            for jm in range(c.NPM):
                wb = s.winr.get()
                s.load_w(('memkv', l, jm), s.win[:, wb, 0:c.NCH * 2 * 128], ('win', wb))

                def evk(ti, t0, n, pb, l=l, jm=jm):
                    f = s.fpr.get()
                    s.cp('dve', s.fp[:, f, 0:n], s.ps[pb][:, 0:n], [('ps', pb)], [('fp', f)])
                    s.dma('sp', s.mkT_p[l, sq, jm * 128:(jm + 1) * 128, :], s.fp[:, f, 0:n], [('fp', f)], [])
                    s.dma('pool', s.mk_sc[l, jm], s.fp[:, f, 0:n], [('fp', f)], [('mksc', l, jm)])

                s.proj_fm(tiles, wb, lambda ch: (ch * 2 + 0) * 128, evk)

                def evv(u0, nu, rows, pb, l=l, jm=jm):
                    f = s.fpr.get()
                    s.cp('dve', s.fp[:, f, 0:nu * 128], s.ps[pb][:, 0:nu * 128], [('ps', pb)], [('fp', f)])
                    src = s.fp[:, f, 0:nu * 128].rearrange("p (u d) -> p u d", d=128)
                    s.dma('sp', s.mv_p[l, sq, jm, u0 * 128:(u0 + nu) * 128, :].rearrange("(u p) d -> p u d", p=128),
                          src, [('fp', f)], [])
                    s.cp('pool', s.mv[:, u0:u0 + nu, jm * 256:jm * 256 + 64], src[:, :, 0:64], [('fp', f)], [('mv',)])
                    s.cp('pool', s.mv[:, u0:u0 + nu, jm * 256 + 192:jm * 256 + 256], src[:, :, 64:128], [('fp', f)],
                         [('mv',)])
                    s.dma('sp', s.mv_sc[l, jm, u0 * 128:(u0 + nu) * 128, :].rearrange("(u p) f -> p u f", p=128),
                          s.mv[:, u0:u0 + nu, jm * 256:(jm + 1) * 256], [('mv',)], [('mvsc', l, jm)])

                s.proj_tm(NM, wb, lambda ch: (ch * 2 + 1) * 128, evv)

    def mixer(s, tiles, l, sq, sample):
        c = s.cfg
        T = c.TS if sample else c.T
        QS = c.PAST if sample else 0
        isA = l < c.LA
        s.norm(tiles, c.gcol('mix', l))
        s.load_mem_kv(l, sample)
        for j in range(c.NPA):
            wb = s.winr.get()
            if isA:
                s.load_w(('ina', l, j), s.win[:, wb, :], ('win', wb))
                qcol = lambda ch: (ch * 3 + 0) * 128
            else:
                s.load_w(('inb', l - c.LA, j), s.win[:, wb, 0:c.NCH * 128], ('win', wb))
                qcol = lambda ch: ch * 128

            def evq(ti, t0, n, pb):
                s.ts('dve', s.qb[:, t0:t0 + n], s.ps[pb][:, 0:n], 0.125, ALU.mult, [('ps', pb)], [('q', ti)])

            s.proj_fm(tiles, wb, qcol, evq)
            ob = s.otr.get()
            if isA:
                KO = c.PAST if sample else 0

                def evk(ti, t0, n, pb, j=j):
                    f = s.fpr.get()
                    s.cp('dve', s.fp[:, f, 0:n], s.ps[pb][:, 0:n], [('ps', pb)], [('fp', f)])
                    dst = (s.akT_s[l, j * 128:(j + 1) * 128, t0:t0 + n] if sample
                           else s.akT_p[l, sq, j * 128:(j + 1) * 128, t0:t0 + n])
                    s.dma('sp', dst, s.fp[:, f, 0:n], [('fp', f)], [])
                    s.cp('pool', s.kb[:, KO + t0:KO + t0 + n], s.fp[:, f, 0:n], [('fp', f)], [('k', (KO + t0) // 512)])

                s.proj_fm(tiles, wb, lambda ch: (ch * 3 + 1) * 128, evk)

                def evv(u0, nu, rows, pb, j=j):
                    f = s.fpr.get()
                    s.cp('dve', s.fp[0:rows, f, 0:nu * 128], s.ps[pb][0:rows, 0:nu * 128], [('ps', pb)], [('fp', f)])
                    src = s.fp[0:rows, f, 0:nu * 128].rearrange("p (u d) -> p u d", d=128)
                    if sample:
                        dst = s.av_s[l, j, 0:rows, :].rearrange("(u p) d -> p u d", p=rows)
                    else:
                        dst = s.av_p[l, sq, j, u0 * 128:(u0 + nu) * 128, :].rearrange("(u p) d -> p u d", p=128)
                    s.dma('sp', dst, src, [('fp', f)], [])
                    kt0 = KO // 128 + u0
                    vt = [('v', kt // 4) for kt in sorted(set([kt0, kt0 + nu - 1]))]
                    s.cp('pool', s.vb[0:rows, kt0:kt0 + nu, 0:64], src[:, :, 0:64], [('fp', f)], vt)
                    s.cp('pool', s.vb[0:rows, kt0:kt0 + nu, 192:256], src[:, :, 64:128], [('fp', f)], vt)

                s.proj_tm(T, wb, lambda ch: (ch * 3 + 2) * 128, evv)
                if sample:
                    ktoks = [('k', i) for i in range((c.PAST + 511) // 512)]
                    s.dma('pool', s.kb[:, 0:c.PAST], s.cakT[l, j * 128:(j + 1) * 128, :], [], ktoks)
                    vtoks = [('v', i) for i in range((c.PAST // 128 + 3) // 4)]
                    for hh in range(2):
                        col = 0 if hh == 0 else 192
                        src = s.cav[l, :, j * 128 + hh * 64: j * 128 + hh * 64 + 64].rearrange("(u p) d -> p u d", p=128)
                        s.dma('pool', s.vb[:, 0:c.PAST // 128, col:col + 64], src, [], vtoks)
                groups = []
                for (ti, t0, n) in tiles:
                    qlo = QS + t0
                    qhi = QS + t0 + n
                    blocks = []
                    KTOT = KO + T
                    k0 = 0
                    while k0 < min(KTOT, qhi):
                        ks = min(128, KTOT - k0)
                        diag = (k0 + ks - 1) >= qlo
                        moff = (XOFF + (qlo - k0)) if diag else None
                        qs = (max(0, k0 - qlo) // 64) * 64 if diag else 0
                        blocks.append((k0, ks, moff, qs))
                        k0 += 128
                    blocks.reverse()
                    groups.append(dict(q0=t0, n=n, qtok=ti, blocks=blocks))
                s.sb_attn(groups, ob)
            else:
                lb = l - c.LA
                gmb = {}
                for hh in range(2):
                    gb = s.gmr.get()
                    gmb[hh] = gb
                    s.dma('pool', s.gm[:, gb, :], s.gm_d[lb, 2 * j + hh], [], [('gm', gb)])
                if sample:
                    nk = c.WBC + T
                    KP0 = c.PAST - c.WBC
                    s.dma('pool', s.kb[:, 0:c.WBC], s.cbkT[j * 128:(j + 1) * 128, :], [],
                          [('k', i) for i in range((c.WBC + 511) // 512)])
                    s.dma('sp', s.kb[:, c.WBC:c.WBC + T], s.kb_sc[c.NSEQ, j, :, 0:T], [('kbsc', c.NSEQ, j)],
                          [('k', c.WBC // 512)])
                    vtoks = [('v', i) for i in range((c.WBC // 128 + 3) // 4)]
                    for hh in range(2):
                        col = 0 if hh == 0 else 192
                        src = s.cbv[:, j * 128 + hh * 64: j * 128 + hh * 64 + 64].rearrange("(u p) d -> p u d", p=128)
                        s.dma('pool', s.vb[:, 0:c.WBC // 128, col:col + 64], src, [], vtoks)
                    s.dma('sp', s.vb[0:T, c.WBC // 128, :], s.vb_sc[c.NSEQ, j, 0:T, :], [('vbsc', c.NSEQ, j)],
                          [('v', (c.WBC // 128) // 4)])
                else:
                    nk = T
                    KP0 = 0
                    s.dma('sp', s.kb[:, 0:T], s.kb_sc[sq, j], [('kbsc', sq, j)], [('k', i) for i in range(T // 512)])
                    s.dma('sp', s.vb[:, 0:T // 128, :], s.vb_sc[sq, j].rearrange("(u p) f -> p u f", p=128),
                          [('vbsc', sq, j)], [('v', i) for i in range((T // 128 + 3) // 4)])
                groups = []
                for (ti, t0, n) in tiles:
                    q0g = QS + t0
                    lo = (q0g // 64 - 8) * 64
                    hi = ((q0g + n - 1) // 64 + 1) * 64
                    blocks = []
                    k0 = 0
                    while k0 < nk:
                        ks = min(128, nk - k0)
                        k0g = KP0 + k0
                        if k0g + ks > lo and k0g < hi:
                            dlt = q0g - k0g
                            assert -384 <= dlt <= 512 and dlt % 64 == 0, dlt
                            blocks.append((k0, ks, XOFF + dlt))
                        k0 += 128
                    groups.append(dict(q0=t0, n=n, qtok=ti, blocks=blocks))
                ksrc = lambda hh, k0, ks: (s.kb[64 * hh:64 * hh + 64, k0:k0 + ks], ('k', k0 // 512))
                vsrc = lambda hh, k0, ks: (s.vb[0:ks, k0 // 128, hh * 128:(hh + 1) * 128], ('v', (k0 // 128) // 4))
                s.sm_attn(groups, ob, ksrc, vsrc, gmb)
            s.out_proj(tiles, l, j, ob)
        for jm in range(c.NPM):
            wkey = ('inam', l, jm) if isA else ('inbm', l - c.LA, jm)
            s.mem_pair(tiles, l, jm, wkey, sample)

    def kvb(s, tiles, sq, sample):
        c = s.cfg
        T = c.TS if sample else c.T
        si = c.NSEQ if sample else sq
        s.norm(tiles, c.gcol('kv'))
        keep0 = 0 if sample else T - c.KEEP
        for j in range(c.NPA):
            wb = s.winr.get()
            s.load_w(('kvb', j), s.win[:, wb, 0:c.NCH * 2 * 128], ('win', wb))

            def evk(ti, t0, n, pb, j=j):
                f = s.fpr.get()
                s.cp('dve', s.fp[:, f, 0:n], s.ps[pb][:, 0:n], [('ps', pb)], [('fp', f)])
                if sample:
                    s.dma('sp', s.bkT_s[j * 128:(j + 1) * 128, t0:t0 + n], s.fp[:, f, 0:n], [('fp', f)], [])
                elif t0 + n > keep0:
                    a = max(t0, keep0)
                    s.dma('sp', s.bkT_p[sq, j * 128:(j + 1) * 128, a - keep0:t0 + n - keep0], s.fp[:, f, a - t0:n],
                          [('fp', f)], [])
                s.dma('pool', s.kb_sc[si, j, :, t0:t0 + n], s.fp[:, f, 0:n], [('fp', f)], [('kbsc', si, j)])

            s.proj_fm(tiles, wb, lambda ch: (ch * 2 + 0) * 128, evk)

            def evv(u0, nu, rows, pb, j=j):
                f = s.fpr.get()
                s.cp('dve', s.fp[0:rows, f, 0:nu * 128], s.ps[pb][0:rows, 0:nu * 128], [('ps', pb)], [('fp', f)])
                src = s.fp[0:rows, f, 0:nu * 128].rearrange("p (u d) -> p u d", d=128)
                if sample:
                    s.dma('sp', s.bv_s[j, 0:rows, :].rearrange("(u p) d -> p u d", p=rows), src, [('fp', f)], [])
                else:
                    for uu in range(nu):
                        tk0 = (u0 + uu) * 128
                        if tk0 >= keep0:
                            s.dma('sp', s.bv_p[sq, j, tk0 - keep0:tk0 - keep0 + 128, :], s.fp[:, f, uu * 128:(uu + 1) * 128],
                                  [('fp', f)], [])
                vt = [('v', kt // 4) for kt in sorted(set([u0, u0 + nu - 1]))]
                s.cp('pool', s.vb[0:rows, u0:u0 + nu, 0:64], src[:, :, 0:64], [('fp', f)], vt)
                s.cp('pool', s.vb[0:rows, u0:u0 + nu, 192:256], src[:, :, 64:128], [('fp', f)], vt)
                if sample:
                    s.dma('sp', s.vb_sc[si, j, 0:rows, :], s.vb[0:rows, 0, :], vt, [('vbsc', si, j)])
                else:
                    s.dma('sp', s.vb_sc[si, j, u0 * 128:(u0 + nu) * 128, :].rearrange("(u p) f -> p u f", p=128),
                          s.vb[:, u0:u0 + nu, :], vt, [('vbsc', si, j)])

            s.proj_tm(T, wb, lambda ch: (ch * 2 + 1) * 128, evv)

    def run_seq(s, sq, sample):
        c = s.cfg
        T = c.TS if sample else c.T
        tiles = [(i, t0, min(512, T - t0)) for i, t0 in enumerate(range(0, T, 512))]
        if not sample:
            s.mem_prep(sq)
        src = s.xT_s if sample else s.xT_p[sq]
        xt = [('x', ch, ti) for ch in range(c.NCH) for (ti, _, _) in tiles]
        s.dma('sp', s.xT[:, :, 0:T], src.rearrange("(c p) t -> p c t", p=128), [], xt)
        for l in range(c.L):
            s.ffn(tiles, l, 1)
            s.mixer(tiles, l, sq, sample)
            s.ffn(tiles, l, 2)
            if l == c.LA - 1:
                s.kvb(tiles, sq, sample)
        gc = c.gcol('final')
        for tile in tiles:
            (ti, t0, n) = tile
            r = s.norm_stats(tile)
            for ch in range(c.NCH):
                f = s.fpr.get()
                s.stt('dve', s.fp[:, f, 0:n], s.xT[:, ch, t0:t0 + n], s.gains[:, gc + ch:gc + ch + 1],
                      s.rs[:, r, 0:n], ALU.mult, ALU.mult, [('x', ch, ti), ('rs', r), ('gains',)], [('fp', f)])
                dst = (s.yT_s[ch * 128:(ch + 1) * 128, t0:t0 + n] if sample
                       else s.yT_p[sq, ch * 128:(ch + 1) * 128, t0:t0 + n])
                s.dma('sp', dst, s.fp[:, f, 0:n], [('fp', f)], [])

    def build(s):
        s.prologue()
        for sq in range(s.cfg.NSEQ):
            s.run_seq(sq, False)
        s.run_seq(0, True)
        s.stats = s.P.emit(s.nc)
        return s.nc


def make_in_maps(cfg, inp):
    c = cfg
    nc_ = c.n_cores
    f = lambda a: np.ascontiguousarray(a, dtype=np.float32)
    wpack = pack_weights(c, inp)
    gains = pack_gains(c, inp)
    consts = make_consts(c)
    gm = make_gm(c, np.asarray(inp['rel_bias_b'], np.float32))
    xp = np.asarray(inp['x_prompt']).reshape(nc_, c.NSEQ, c.T, c.D)
    mp = np.asarray(inp['mem_prompt']).reshape(nc_, c.NSEQ, c.NMEM, c.D)
    maps = []
    for k in range(nc_):
        m = {
            'xT_p': f(xp[k].transpose(0, 2, 1)),
            'memT_p': f(mp[k].transpose(0, 2, 1)),
            'xT_s': f(np.asarray(inp['x_sample'])[k].T),
            'cakT': f(np.asarray(inp['cache_a_k'])[:, k].reshape(c.LA, c.PAST, c.AW).transpose(0, 2, 1)),
            'cav': f(np.asarray(inp['cache_a_v'])[:, k].reshape(c.LA, c.PAST, c.AW)),
            'cbkT': f(np.asarray(inp['cache_b_k'])[k].reshape(c.WBC, c.AW).T),
            'cbv': f(np.asarray(inp['cache_b_v'])[k].reshape(c.WBC, c.AW)),
            'cmkT': f(np.asarray(inp['cache_mem_k'])[:, k].reshape(c.L, c.NMEM, c.MW).transpose(0, 2, 1)),
            'cmv': f(np.asarray(inp['cache_mem_v'])[:, k].reshape(c.L, c.NMEM, c.MW)),
            'gains': gains, 'consts': consts, 'gm': gm, 'wpack': wpack,
        }
        maps.append(m)
    return maps


def assemble(cfg, results):
    c = cfg
    cat = lambda name: np.stack([np.asarray(r[name]) for r in results])
    B = c.n_cores * c.NSEQ
    y_prompt = cat('yT_p').transpose(0, 1, 3, 2).reshape(B, c.T, c.D)
    y_sample = cat('yT_s').transpose(0, 2, 1)
    ak = cat('akT_p').transpose(1, 0, 2, 4, 3).reshape(c.LA, B, c.T, c.HA, 64)
    av = cat('av_p').transpose(1, 0, 2, 4, 3, 5).reshape(c.LA, B, c.T, c.HA, 64)
    bk = cat('bkT_p').transpose(0, 1, 3, 2).reshape(B, c.KEEP, c.HA, 64)
    bv = cat('bv_p').transpose(0, 1, 3, 2, 4).reshape(B, c.KEEP, c.HA, 64)
    mk = cat('mkT_p').transpose(1, 0, 2, 4, 3).reshape(c.L, B, c.NMEM, c.HM, 64)
    mv = cat('mv_p').transpose(1, 0, 2, 4, 3, 5).reshape(c.L, B, c.NMEM, c.HM, 64)
    aks = cat('akT_s').transpose(1, 0, 3, 2).reshape(c.LA, c.n_cores, c.TS, c.HA, 64)
    avs = cat('av_s').transpose(1, 0, 3, 2, 4).reshape(c.LA, c.n_cores, c.TS, c.HA, 64)
    bks = cat('bkT_s').transpose(0, 2, 1).reshape(c.n_cores, c.TS, c.HA, 64)
    bvs = cat('bv_s').transpose(0, 2, 1, 3).reshape(c.n_cores, c.TS, c.HA, 64)
    outs = (y_prompt, y_sample, ak, av, bk, bv, mk, mv, aks, avs, bks, bvs)
    return tuple(np.ascontiguousarray(o, dtype=np.float32) for o in outs)


def run(cfg, inp, trace=False):
    b = Builder(cfg)
    nc = b.build()
    maps = make_in_maps(cfg, inp)
    res = run_bass_kernel_spmd(nc, maps, core_ids=list(range(cfg.n_cores)), trace=trace)
    return assemble(cfg, res.results), res, b


def kernel(**inputs):
    cfg = Cfg()
    outs, _, _ = run(cfg, inputs)
    return outs
```

```python
import numpy as np
import concourse.bass as bass
import concourse.mybir as mybir
from concourse.bass_utils import run_bass_kernel_spmd

F32 = mybir.dt.float32
BF16 = mybir.dt.bfloat16
AF = mybir.ActivationFunctionType
ALU = mybir.AluOpType
NEG = -30000.0
EPS = 1e-6
XOFF = 384
GMW = 1408
NMW = 896
WCH = 8192
LOOKAHEAD = 6


class Cfg:
    def __init__(s, D=1024, T=2048, NSEQ=4, DFF=2816, HA=12, HM=4, NMEM=256, PAST=1024, TS=32, L=4, G=2,
                 WB=512, n_cores=8):
        s.D, s.T, s.NSEQ, s.DFF, s.HA, s.HM, s.NMEM, s.PAST, s.TS, s.L, s.G, s.WB = D, T, NSEQ, DFF, HA, HM, NMEM, PAST, TS, L, G, WB
        s.n_cores = n_cores
        s.NCH = D // 128
        s.NF = DFF // 128
        assert s.NF % G == 0
        s.NG = s.NF // G
        s.NPA = HA // 2
        s.NPM = HM // 2
        s.LA = L // 2
        s.LB = L - s.LA
        s.AW = HA * 64
        s.MW = HM * 64
        assert (HA + HM) * 64 == D
        s.WBC = min(WB, PAST)
        s.KEEP = min(WB, T)
        s.KT = max(T, PAST + TS)
        s.NKT = (s.KT + 127) // 128
        s.NGC = (4 * L + 2) * s.NCH

    def gcol(s, name, l=0):
        order = {'ff1': 0, 'mix': 1, 'ff2': 2, 'mem': 3}
        if name == 'kv':
            return 4 * s.L * s.NCH
        if name == 'final':
            return (4 * s.L + 1) * s.NCH
        return (order[name] * s.L + l) * s.NCH


def weight_dir(cfg):
    blocks = []
    NCH, G, D = cfg.NCH, cfg.G, cfg.D
    for l in range(cfg.L):
        for jm in range(cfg.NPM):
            blocks.append((('memkv', l, jm), NCH * 2 * 128))
    for l in range(cfg.L):
        for g in range(cfg.NG):
            blocks.append((('gu', l, 1, g), NCH * 2 * G * 128))
            blocks.append((('dn', l, 1, g), G * D))
        if l < cfg.LA:
            for j in range(cfg.NPA):
                blocks.append((('ina', l, j), NCH * 3 * 128))
                blocks.append((('out', l, j), D))
            for jm in range(cfg.NPM):
                blocks.append((('inam', l, jm), NCH * 128))
                blocks.append((('out', l, cfg.NPA + jm), D))
        else:
            for j in range(cfg.NPA):
                blocks.append((('inb', l - cfg.LA, j), NCH * 128))
                blocks.append((('out', l, j), D))
            for jm in range(cfg.NPM):
                blocks.append((('inbm', l - cfg.LA, jm), NCH * 128))
                blocks.append((('out', l, cfg.NPA + jm), D))
        for g in range(cfg.NG):
            blocks.append((('gu', l, 2, g), NCH * 2 * G * 128))
            blocks.append((('dn', l, 2, g), G * D))
        if l == cfg.LA - 1:
            for j in range(cfg.NPA):
                blocks.append((('kvb', j), NCH * 2 * 128))
    d = {}
    off = 0
    for k, F in blocks:
        d[k] = (off, F)
        off += 128 * F
    chunk = 128 * WCH
    total = ((off + chunk - 1) // chunk) * chunk
    return blocks, d, total


def pack_weights(cfg, w):
    blocks, d, total = weight_dir(cfg)
    out = np.zeros(total, np.float32)
    NCH, G, D, DFF, AW = cfg.NCH, cfg.G, cfg.D, cfg.DFF, cfg.AW

    def kc(m):
        return m.reshape(NCH, 128, -1).transpose(1, 0, 2)

    for key, F in blocks:
        kind = key[0]
        if kind == 'gu':
            _, l, i, g = key
            m = w['w_ff1_gu' if i == 1 else 'w_ff2_gu'][l]
            f0 = g * G * 128
            blk = np.stack([kc(m[:, f0:f0 + G * 128]), kc(m[:, DFF + f0:DFF + f0 + G * 128])], axis=2)
        elif kind == 'dn':
            _, l, i, g = key
            m = w['w_ff1_down' if i == 1 else 'w_ff2_down'][l]
            blk = m[g * G * 128:(g + 1) * G * 128].reshape(G, 128, D).transpose(1, 0, 2)
        elif kind == 'ina':
            _, l, j = key
            m = w['w_in_a'][l]
            blk = np.stack([kc(m[:, s * AW + j * 128: s * AW + (j + 1) * 128]) for s in range(3)], axis=2)
        elif kind == 'inam':
            _, l, jm = key
            m = w['w_in_a'][l]
            blk = kc(m[:, 3 * AW + jm * 128: 3 * AW + (jm + 1) * 128])
        elif kind == 'inb':
            _, lb, j = key
            blk = kc(w['w_in_b'][lb][:, j * 128:(j + 1) * 128])
        elif kind == 'inbm':
            _, lb, jm = key
            blk = kc(w['w_in_b'][lb][:, AW + jm * 128: AW + (jm + 1) * 128])
        elif kind == 'out':
            _, l, j = key
            blk = w['w_out'][l][j * 128:(j + 1) * 128]
        elif kind == 'memkv':
            _, l, jm = key
            m = w['w_mem_kv'][l]
            blk = np.stack([kc(m[:, s * cfg.MW + jm * 128: s * cfg.MW + (jm + 1) * 128]) for s in range(2)], axis=2)
        elif kind == 'kvb':
            _, j = key
            m = w['w_kv_b']
            blk = np.stack([kc(m[:, s * AW + j * 128: s * AW + (j + 1) * 128]) for s in range(2)], axis=2)
        off = d[key][0]
        out[off:off + 128 * F] = np.ascontiguousarray(blk, dtype=np.float32).reshape(-1)
    return out


def pack_gains(cfg, w):
    cols = []
    for name in ('g_ff1', 'g_mix', 'g_ff2', 'g_mem'):
        for l in range(cfg.L):
            cols.append(w[name][l].reshape(cfg.NCH, 128).T)
    cols.append(w['g_kv'].reshape(cfg.NCH, 128).T)
    cols.append(w['g_final'].reshape(cfg.NCH, 128).T)
    return np.ascontiguousarray(np.concatenate(cols, axis=1), dtype=np.float32)


def make_consts(cfg):
    k = np.arange(128)[:, None]
    x = np.arange(NMW)[None, :] - XOFF
    nm = np.where(k >= x, NEG, 0.0).astype(np.float32)
    ident = np.eye(128, dtype=np.float32)
    j = np.arange(128)[:, None]
    s = np.arange(128)[None, :]
    negu = np.where(j >= s, -1.0, 0.0).astype(np.float32)
    negones = -np.ones((128, 128), np.float32)
    onesmean = np.full((128, 128), 1.0 / cfg.D, np.float32)
    return np.ascontiguousarray(np.concatenate([nm, ident, negu, negones, onesmean], axis=1))


def make_gm(cfg, rel_bias_b):
    k = np.arange(128)[:, None]
    x = np.arange(GMW)[None, :] - XOFF
    rel = np.clip(x - k, -128, 128) + 128
    xc = np.floor_divide(x, 64)
    kcn = k // 64
    vis = (kcn <= xc) & (xc <= kcn + 8)
    g = rel_bias_b[:, :, rel]
    return np.ascontiguousarray(np.where(vis[None, None], g, np.float32(NEG)), dtype=np.float32)


class Op:
    __slots__ = ('eng', 'fn', 'deps', 'dma', 'sig', 'sem', 'val', 'slot_prev')

    def __init__(s, eng, fn, dma):
        s.eng, s.fn, s.dma = eng, fn, dma
        s.deps = []
        s.sig = dma
        s.sem = None
        s.val = 0
        s.slot_prev = None


ENGS = ('pe', 'act', 'dve', 'pool', 'sp')


class Prog:
    def __init__(s):
        s.ops = []
        s.lw = {}
        s.rd = {}

    def add(s, eng, fn, reads=(), writes=(), dma=False):
        op = Op(eng, fn, dma)
        oid = len(s.ops)
        deps = set()
        for r in reads:
            w = s.lw.get(r)
            if w is not None:
                deps.add(w)
        for wt in writes:
            w = s.lw.get(wt)
            if w is not None:
                wo = s.ops[w]
                if wo.dma or dma or wo.eng != eng:
                    deps.add(w)
            rdd = s.rd.get(wt)
            if rdd:
                for (reng, rdma, _), rid in rdd.items():
                    if rdma or dma or reng != eng:
                        deps.add(rid)
        final = []
        for dd in deps:
            do = s.ops[dd]
            if (not do.dma) and (not dma) and do.eng == eng and eng == 'pe':
                continue
            final.append(dd)
            do.sig = True
        op.deps = final
        s.ops.append(op)
        for r in reads:
            dct = s.rd.setdefault(r, {})
            if dma:
                dct[(eng, True, oid)] = oid
            else:
                dct[(eng, False, 0)] = oid
        for wt in writes:
            s.lw[wt] = oid
            s.rd[wt] = {}
        return oid

    def emit(s, nc, n_dma_sems=None):
        n_dma_sems = n_dma_sems or {'sp': 24, 'pool': 48}
        csem = {e: nc.alloc_semaphore('s_' + e) for e in ('pe', 'act', 'dve', 'pool')}
        dsem = {q: [nc.alloc_semaphore('d_%s%d' % (q, i)) for i in range(n)] for q, n in n_dma_sems.items()}
        cnt = {e: 0 for e in csem}
        dcount = {q: 0 for q in dsem}
        slot_cnt = {q: [0] * len(dsem[q]) for q in dsem}
        for op in s.ops:
            if op.dma:
                q = op.eng
                i = dcount[q] % len(dsem[q])
                dcount[q] += 1
                op.slot_prev = slot_cnt[q][i]
                slot_cnt[q][i] += 16
                op.sem = dsem[q][i]
                op.val = slot_cnt[q][i]
            elif op.sig:
                cnt[op.eng] += 1
                op.sem = csem[op.eng]
                op.val = cnt[op.eng]
        per = {e: [] for e in ENGS}
        for op in s.ops:
            per[op.eng].append(op)
        ops = s.ops
        final_waits = []
        for q in dsem:
            for i, sm in enumerate(dsem[q]):
                if slot_cnt[q][i] > 0:
                    final_waits.append((sm, slot_cnt[q][i]))
        stats = {e: [len(per[e]), 0] for e in ENGS}

        def run(ename, e):
            waited = {}
            nw = 0
            for op in per[ename]:
                waits = {}
                for dd in op.deps:
                    do = ops[dd]
                    k = id(do.sem)
                    if k not in waits or waits[k][1] < do.val:
                        waits[k] = (do.sem, do.val)
                if op.dma and op.slot_prev:
                    k = id(op.sem)
                    if k not in waits or waits[k][1] < op.slot_prev:
                        waits[k] = (op.sem, op.slot_prev)
                for k, (sm, v) in waits.items():
                    if waited.get(k, 0) < v:
                        e.wait_ge(sm, v)
                        waited[k] = v
                        nw += 1
                ins = op.fn(e)
                if op.sig:
                    ins.then_inc(op.sem, 16 if op.dma else 1)
            if ename == 'sp':
                for sm, v in final_waits:
                    if waited.get(id(sm), 0) < v:
                        e.wait_ge(sm, v)
            stats[ename][1] = nw

        with nc.Block() as block:
            @block.tensor
            def _(e):
                run('pe', e)

            @block.scalar
            def _(e):
                run('act', e)

            @block.vector
            def _(e):
                run('dve', e)

            @block.gpsimd
            def _(e):
                run('pool', e)

            @block.sync
            def _(e):
                run('sp', e)
        return stats


class Rot:
    def __init__(s, items):
        s.items = list(items)
        s.i = 0

    def get(s):
        v = s.items[s.i % len(s.items)]
        s.i += 1
        return v


class Builder:
    def __init__(s, cfg):
        s.cfg = cfg
        s.P = Prog()
        s.nc = bass.Bass("TRN2", target_bir_lowering=False)
        s.blocks, s.wdir, s.wtotal = weight_dir(cfg)
        s._decl()

    def _decl(s):
        nc, c = s.nc, s.cfg
        di = lambda n, sh: nc.dram_tensor(n, sh, F32, kind="ExternalInput").ap()
        do = lambda n, sh: nc.dram_tensor(n, sh, F32, kind="ExternalOutput").ap()
        s.xT_p = di("xT_p", [c.NSEQ, c.D, c.T])
        s.memT_p = di("memT_p", [c.NSEQ, c.D, c.NMEM])
        s.xT_s = di("xT_s", [c.D, c.TS])
        s.cakT = di("cakT", [c.LA, c.AW, c.PAST])
        s.cav = di("cav", [c.LA, c.PAST, c.AW])
        s.cbkT = di("cbkT", [c.AW, c.WBC])
        s.cbv = di("cbv", [c.WBC, c.AW])
        s.cmkT = di("cmkT", [c.L, c.MW, c.NMEM])
        s.cmv = di("cmv", [c.L, c.NMEM, c.MW])
        s.gains_d = di("gains", [128, c.NGC])
        s.consts_d = di("consts", [128, GMW])
        s.gm_d = di("gm", [c.LB, c.HA, 128, GMW])
        s.wpack = di("wpack", [s.wtotal])
        s.yT_p = do("yT_p", [c.NSEQ, c.D, c.T])
        s.yT_s = do("yT_s", [c.D, c.TS])
        s.akT_p = do("akT_p", [c.LA, c.NSEQ, c.AW, c.T])
        s.av_p = do("av_p", [c.LA, c.NSEQ, c.NPA, c.T, 128])
        s.bkT_p = do("bkT_p", [c.NSEQ, c.AW, c.KEEP])
        s.bv_p = do("bv_p", [c.NSEQ, c.NPA, c.KEEP, 128])
        s.mkT_p = do("mkT_p", [c.L, c.NSEQ, c.MW, c.NMEM])
        s.mv_p = do("mv_p", [c.L, c.NSEQ, c.NPM, c.NMEM, 128])
        s.akT_s = do("akT_s", [c.LA, c.AW, c.TS])
        s.av_s = do("av_s", [c.LA, c.NPA, c.TS, 128])
        s.bkT_s = do("bkT_s", [c.AW, c.TS])
        s.bv_s = do("bv_s", [c.NPA, c.TS, 128])
        ds = lambda n, sh: nc.dram_tensor(n, sh, BF16, kind="Internal").ap()
        s.wsc = ds("wsc", [s.wtotal])
        s.kb_sc = ds("kb_sc", [c.NSEQ + 1, c.NPA, 128, c.T])
        s.vb_sc = ds("vb_sc", [c.NSEQ + 1, c.NPA, c.T, 256])
        s.mk_sc = ds("mk_sc", [c.L, c.NPM, 128, c.NMEM])
        s.mv_sc = ds("mv_sc", [c.L, c.NPM, c.NMEM, 256])
        sb = lambda n, sh, dt: nc.alloc_sbuf_tensor(n, sh, dt)
        s.xT = sb("xT", [128, c.NCH, c.T], F32)
        s.hT = sb("hT", [128, c.NCH, c.T], BF16)
        s.wgu = sb("wgu", [128, 2, c.NCH * 2 * c.G * 128], BF16)
        s.wd = sb("wd", [128, 2, c.G * c.D], BF16)
        s.win = sb("win", [128, 2, c.NCH * 3 * 128], BF16)
        s.wout = sb("wout", [128, 2, c.D], BF16)
        s.qb = sb("qb", [128, c.T], BF16)
        s.kb = sb("kb", [128, c.KT], BF16)
        s.vb = sb("vb", [128, c.NKT, 256], BF16)
        s.oT = sb("oT", [128, 2, c.T], BF16)
        s.NFP, s.NBP = 6, 12
        s.fp = sb("fp", [128, s.NFP, 512], F32)
        s.bp = sb("bp", [128, s.NBP, 512], BF16)
        s.rs = sb("rs", [128, 2, 512], F32)
        s.rsr = Rot([0, 1])
        s.gm = sb("gmb", [128, 2, GMW], BF16)
        s.cst = sb("cst", [128, GMW], BF16)
        s.gains = sb("gains_sb", [128, c.NGC], F32)
        s.mk = sb("mk", [128, c.NPM, c.NMEM], BF16)
        s.mv = sb("mv", [128, c.NMEM // 128, c.NPM * 256], BF16)
        s.pd = [nc.alloc_psum_tensor("pd%d" % i, [128, 2, 512], F32) for i in range(4)]
        s.ps = [s.pd[b // 2][:, b % 2, :] for b in range(8)]
        s.lac = sb("lac", [128, 2, 2, 512], BF16)
        s.lacr = Rot([0, 1])
        s.zdr = Rot([0, 1, 3])
        s.fdr = Rot(range(s.NFP // 2))
        s.bdr = Rot(range(s.NBP // 2))
        s.opq = []
        s.ovr_a = Rot([5])
        s.zdr2 = Rot([0, 1])
        s.opr = Rot([0, 1, 2, 3, 6, 7])
        s.fpr = Rot(range(s.NFP))
        s.bpr = Rot(range(s.NBP))
        s.zr = Rot([0, 1, 2, 3])
        s.orr = Rot([4, 5])
        s.mr = Rot([6, 7])
        s.gr = Rot([0, 1])
        s.ur = Rot([2, 3])
        s.dr = Rot([4, 5, 6, 7])
        s.wgur = Rot([0, 1])
        s.winr = Rot([0, 1])
        s.woutr = Rot([0, 1])
        s.otr = Rot([0, 1])
        s.gmr = Rot([0, 1])
        class BS:
            pass
        b0 = BS()
        b0.id, b0.q, b0.k, b0.v, b0.xt = 0, s.qb, s.kb, s.vb, []
        half = (c.NCH * 2 * c.G * 128) // 2
        assert c.T <= half and c.KT <= half and c.NKT * 256 <= 2 * half
        b1 = BS()
        b1.id = 1
        b1.q = s.wgu[:, 0, 0:half]
        b1.k = s.wgu[:, 0, half:2 * half]
        b1.v = s.wgu[:, 1, 0:c.NKT * 256].rearrange("p (k f) -> p k f", f=256)
        b1.xt = [('wgu', 0), ('wgu', 1)]
        s.bsets = [b0, b1]
        s.nm = s.cst[:, 0:NMW]
        s.ident = s.cst[:, NMW:NMW + 128]
        s.negu = s.cst[:, NMW + 128:NMW + 256]
        s.negones = s.cst[:, NMW + 256:NMW + 384]
        s.onesmean = s.cst[:, NMW + 384:NMW + 512]

    def mm(s, out, lhsT, rhs, start, stop, reads, writes):
        s.P.add('pe', lambda e: e.matmul(out, lhsT=lhsT, rhs=rhs, start=start, stop=stop, skip_group_check=True),
                reads, writes)

    def act(s, out, in_, func, reads, writes, scale=1.0, bias=0.0):
        s.P.add('act', lambda e: e.activation(out=out, in_=in_, func=func, bias=bias, scale=scale), reads, writes)

    def tt(s, eng, out, in0, in1, op, reads, writes):
        s.P.add(eng, lambda e: e.tensor_tensor(out=out, in0=in0, in1=in1, op=op), reads, writes)

    def ts(s, eng, out, in0, sc, op, reads, writes):
        s.P.add(eng, lambda e: e.tensor_scalar(out=out, in0=in0, scalar1=sc, scalar2=None, op0=op), reads, writes)

    def stt(s, eng, out, in0, sc, in1, op0, op1, reads, writes):
        s.P.add(eng, lambda e: e.scalar_tensor_tensor(out=out, in0=in0, scalar=sc, in1=in1, op0=op0, op1=op1),
                reads, writes)

    def cp(s, eng, out, in_, reads, writes):
        s.P.add(eng, lambda e: e.tensor_copy(out=out, in_=in_), reads, writes)

    def dma(s, q, out, in_, reads, writes):
        s.P.add(q, lambda e: e.dma_start(out=out, in_=in_), reads, writes, dma=True)

    def wblock(s, key):
        off, F = s.wdir[key]
        return s.wsc[off:off + 128 * F].rearrange("(p f) -> p f", p=128), F, off

    def wtok(s, off, F):
        ch = 128 * WCH
        return [('wc', k) for k in range(off // ch, (off + 128 * F - 1) // ch + 1)]

    def ensure_chunks(s, upto):
        nchunks = s.wtotal // (128 * WCH)
        src = s.wpack.rearrange("(k p f) -> k p f", p=128, f=WCH)
        dst = s.wsc.rearrange("(k p f) -> k p f", p=128, f=WCH)
        while s.chunk_issued <= min(upto, nchunks - 1):
            k = s.chunk_issued
            s.dma('pool', dst[k], src[k], [], [('wc', k)])
            s.chunk_issued += 1

    def load_w(s, key, dst, dtok):
        ap, F, off = s.wblock(key)
        toks = s.wtok(off, F)
        s.ensure_chunks(toks[-1][1] + LOOKAHEAD)
        s.dma('sp', dst, ap, toks, [dtok])

    def prologue(s):
        c = s.cfg
        s.dma('pool', s.cst[:], s.consts_d, [], [('cst',)])
        s.dma('sp', s.gains[:], s.gains_d, [], [('gains',)])
        s.chunk_issued = 0
        s.ensure_chunks(LOOKAHEAD)
        s.P.add('pool', lambda e: e.memset(s.vb[:, :, 64:192], 1.0), [], [('v', 0, i) for i in range((c.NKT + 3) // 4)])
        for jm in range(c.NPM):
            s.P.add('pool', lambda e, jm=jm: e.memset(s.mv[:, :, jm * 256 + 64: jm * 256 + 192], 1.0), [], [('mv',)])

    def norm_stats(s, tile, src_tok='x'):
        c = s.cfg
        (ti, t0, n) = tile
        pb = s.mr.get()
        for ch in range(c.NCH):
            b = s.bpr.get()
            s.act(s.bp[:, b, 0:n], s.xT[:, ch, t0:t0 + n], AF.Square, [(src_tok, ch, ti)], [('bp', b)])
            s.mm(s.ps[pb][:, 0:n], s.onesmean, s.bp[:, b, 0:n], ch == 0, ch == c.NCH - 1,
                 [('bp', b), ('cst',)], [('ps', pb)])
        f1 = s.fpr.get()
        s.act(s.fp[:, f1, 0:n], s.ps[pb][:, 0:n], AF.Ln, [('ps', pb)], [('fp', f1)], bias=EPS)
        r = s.rsr.get()
        s.act(s.rs[:, r, 0:n], s.fp[:, f1, 0:n], AF.Exp, [('fp', f1)], [('rs', r)], scale=-0.5)
        return r

    def norm_apply(s, tile, r, gc, htok='h'):
        c = s.cfg
        (ti, t0, n) = tile
        for ch in range(c.NCH):
            s.stt('dve', s.hT[:, ch, t0:t0 + n], s.xT[:, ch, t0:t0 + n], s.gains[:, gc + ch:gc + ch + 1],
                  s.rs[:, r, 0:n], ALU.mult, ALU.mult,
                  [('x', ch, ti), ('rs', r), ('gains',)], [(htok, ch, ti)])

    def norm(s, tiles, gc):
        for tile in tiles:
            r = s.norm_stats(tile)
            s.norm_apply(tile, r, gc)

    def resid(s, dc, ti, t0, n, pd, scale, path):
        if path == 0:
            if scale == 1.0:
                s.tt('dve', s.xT[:, dc, t0:t0 + n], s.ps[pd][:, 0:n], s.xT[:, dc, t0:t0 + n], ALU.add,
                     [('ps', pd), ('x', dc, ti)], [('x', dc, ti)])
            else:
                s.stt('dve', s.xT[:, dc, t0:t0 + n], s.ps[pd][:, 0:n], scale, s.xT[:, dc, t0:t0 + n], ALU.mult,
                      ALU.add, [('ps', pd), ('x', dc, ti)], [('x', dc, ti)])
        else:
            f = s.fpr.get()
            s.act(s.fp[:, f, 0:n], s.ps[pd][:, 0:n], AF.Copy, [('ps', pd)], [('fp', f)], scale=scale)
            s.tt('pool', s.xT[:, dc, t0:t0 + n], s.xT[:, dc, t0:t0 + n], s.fp[:, f, 0:n], ALU.add,
                 [('fp', f), ('x', dc, ti)], [('x', dc, ti)])

    def ffn(s, tiles, l, i):
        c = s.cfg
        G, NCH = c.G, c.NCH
        gc = c.gcol('ff1' if i == 1 else 'ff2', l)
        nt = len(tiles)

        def do_norm(k):
            if k < nt:
                r = s.norm_stats(tiles[k])
                s.norm_apply(tiles[k], r, gc)

        def load(g):
            b = s.wgur.get()
            s.load_w(('gu', l, i, g), s.wgu[:, b, :], ('wgu', b))
            s.load_w(('dn', l, i, g), s.wd[:, b, :], ('wd', b))
            return b

        def GU(b, tile):
            (ti, t0, n) = tile
            acts = []
            for fi in range(G):
                pg, pu = s.gr.get(), s.ur.get()
                for ch in range(NCH):
                    base = (ch * 2 + 0) * G * 128 + fi * 128
                    s.mm(s.ps[pg][:, 0:n], s.wgu[:, b, base:base + 128], s.hT[:, ch, t0:t0 + n], ch == 0,
                         ch == NCH - 1, [('wgu', b), ('h', ch, ti)], [('ps', pg)])
                for ch in range(NCH):
                    base = (ch * 2 + 1) * G * 128 + fi * 128
                    s.mm(s.ps[pu][:, 0:n], s.wgu[:, b, base:base + 128], s.hT[:, ch, t0:t0 + n], ch == 0,
                         ch == NCH - 1, [('wgu', b), ('h', ch, ti)], [('ps', pu)])
                f = s.fpr.get()
                s.act(s.fp[:, f, 0:n], s.ps[pg][:, 0:n], AF.Silu, [('ps', pg)], [('fp', f)])
                a = s.bpr.get()
                s.tt('dve', s.bp[:, a, 0:n], s.ps[pu][:, 0:n], s.fp[:, f, 0:n], ALU.mult,
                     [('ps', pu), ('fp', f)], [('bp', a)])
                acts.append(a)
            return acts

        def DOWN(b, tile, acts):
            (ti, t0, n) = tile
            for dc in range(NCH):
                pd = s.dr.get()
                for fi in range(G):
                    s.mm(s.ps[pd][:, 0:n], s.wd[:, b, fi * c.D + dc * 128: fi * c.D + (dc + 1) * 128],
                         s.bp[:, acts[fi], 0:n], fi == 0, fi == G - 1, [('wd', b), ('bp', acts[fi])], [('ps', pd)])
                s.resid(dc, ti, t0, n, pd, 0.5, dc % 2)

        do_norm(0)
        do_norm(1)
        bufs = {0: load(0)}
        steps = [(g, k) for g in range(c.NG) for k in range(nt)]
        prev = None
        for (g, k) in steps:
            acts = GU(bufs[g], tiles[k])
            if prev is not None:
                DOWN(*prev)
            prev = (bufs[g], tiles[k], acts)
            if k == 0 and g + 1 < c.NG:
                bufs[g + 1] = load(g + 1)
            if g == 0:
                do_norm(k + 2)
        DOWN(*prev)

    def proj_fm(s, tiles, wbuf, colf, evac, rot=None):
        for ch in s.proj_fm_chunks(tiles, wbuf, colf, evac, rot):
            ch()

    def proj_fm_chunks(s, tiles, wbuf, colf, evac, rot=None):
        c = s.cfg
        rot = rot or s.mr
        out = []
        for (ti, t0, n) in tiles:
            def chunk(ti=ti, t0=t0, n=n):
                pb = rot.get()
                for ch in range(c.NCH):
                    cb = colf(ch)
                    s.mm(s.ps[pb][:, 0:n], s.win[:, wbuf, cb:cb + 128], s.hT[:, ch, t0:t0 + n], ch == 0,
                         ch == c.NCH - 1, [('win', wbuf), ('h', ch, ti)], [('ps', pb)])
                evac(ti, t0, n, pb)
            out.append(chunk)
        return out

    def proj_tm(s, T, wbuf, colf, evac, rot=None):
        for ch in s.proj_tm_chunks(T, wbuf, colf, evac, rot):
            ch()

    def proj_tm_chunks(s, T, wbuf, colf, evac, rot=None):
        c = s.cfg
        rot = rot or s.mr
        ntt = (T + 127) // 128
        out = []
        for u0 in range(0, ntt, 4):
            def chunk(u0=u0):
                nu = min(4, ntt - u0)
                pb = rot.get()
                rows = min(128, T - u0 * 128)
                for uu in range(nu):
                    tk0 = (u0 + uu) * 128
                    r = min(128, T - tk0)
                    for ch in range(c.NCH):
                        cb = colf(ch)
                        s.mm(s.ps[pb][0:r, uu * 128:(uu + 1) * 128], s.hT[:, ch, tk0:tk0 + r],
                             s.win[:, wbuf, cb:cb + 128], ch == 0, ch == c.NCH - 1,
                             [('win', wbuf), ('h', ch, tk0 // 512)], [('ps', pb)])
                evac(u0, nu, rows, pb)
            out.append(chunk)
        return out

    def sb_attn(s, groups, ob, bs, hook=None):
        OD = 2
        items = []
        for g in groups:
            nb = len(g['blocks'])
            g['lac'] = s.lacr.get()
            for bi, (k0, ks, moff, qs) in enumerate(g['blocks']):
                items.append(dict(g=g, k0=k0, ks=ks, moff=moff, qs=qs, first=(bi == 0), last=(bi == nb - 1)))
        n_it = len(items)
        pdt = lambda d: [('ps', 2 * d), ('ps', 2 * d + 1)]
        bpt = lambda d: [('bp', 2 * d), ('bp', 2 * d + 1)]
        fpt = lambda d: [('fp', 2 * d), ('fp', 2 * d + 1)]

        def A(it):
            g = it['g']
            q0, n, ks, k0, qs = g['q0'], g['n'], it['ks'], it['k0'], it['qs']
            zd = s.zdr.get()
            it['zd'] = zd
            if it['first']:
                lc = g['lac']
                s.P.add('pool', lambda e: e.memset(s.lac[:, lc, :, 0:n], 0.0), [], [('lac', lc)])
            for hh in range(2):
                s.mm(s.pd[zd][0:ks, hh, qs:n], bs.k[64 * hh:64 * hh + 64, k0:k0 + ks],
                     bs.q[64 * hh:64 * hh + 64, q0 + qs:q0 + n], True, False,
                     [('k', bs.id, k0 // 512), ('q', bs.id, g['qtok'])] + bs.xt, [('ps', 2 * zd + hh)])
            if it['moff'] is not None:
                mo = it['moff']
                for hh in range(2):
                    s.mm(s.pd[zd][0:ks, hh, qs:n], s.ident[0:ks, 0:ks], s.nm[0:ks, mo + qs:mo + n], False, False,
                         [('cst',)], [('ps', 2 * zd + hh)])
            e = s.fdr.get()
            s.act(s.fp[0:ks, 2 * e:2 * e + 2, qs:n], s.pd[zd][0:ks, :, qs:n], AF.Exp, pdt(zd), fpt(e))
            sp = s.bdr.get()
            it['sp'] = sp
            s.act(s.bp[0:ks, 2 * sp:2 * sp + 2, qs:n], s.fp[0:ks, 2 * e:2 * e + 2, qs:n], AF.Ln, fpt(e), bpt(sp),
                  bias=1.0)

        def B(it):
            g = it['g']
            n, ks, zd, sp, qs = g['n'], it['ks'], it['zd'], it['sp'], it['qs']
            first = it['first']
            lc = g['lac']
            for hh in range(2):
                s.mm(s.pd[zd][0:ks, hh, qs:n], s.negu[0:ks, 0:ks], s.bp[0:ks, 2 * sp + hh, qs:n], False, first,
                     [('bp', 2 * sp + hh), ('cst',)], [('ps', 2 * zd + hh)])
            if not first:
                for hh in range(2):
                    s.mm(s.pd[zd][0:ks, hh, qs:n], s.negones[:, 0:ks], s.lac[:, lc, hh, qs:n], False, True,
                         [('lac', lc), ('cst',)], [('ps', 2 * zd + hh)])
            if not it['last']:
                s.tt('pool', s.lac[0:ks, lc, :, qs:n], s.lac[0:ks, lc, :, qs:n], s.bp[0:ks, 2 * sp:2 * sp + 2, qs:n],
                     ALU.add, [('lac', lc)] + bpt(sp), [('lac', lc)])
            w = s.bdr.get()
            it['w'] = w
            s.act(s.bp[0:ks, 2 * w:2 * w + 2, qs:n], s.pd[zd][0:ks, :, qs:n], AF.Exp, pdt(zd), bpt(w))

        def C(it):
            g = it['g']
            q0, n, ks, k0, w, qs = g['q0'], g['n'], it['ks'], it['k0'], it['w'], it['qs']
            kt = k0 // 128
            for hh in range(2):
                s.mm(s.pd[OD][:, 0, qs:n], bs.v[0:ks, kt, hh * 128:(hh + 1) * 128], s.bp[0:ks, 2 * w + hh, qs:n],
                     it['first'] and hh == 0, it['last'] and hh == 1,
                     [('v', bs.id, kt // 4), ('bp', 2 * w + hh)] + bs.xt, [('ps', 2 * OD)])
            if it['last']:
                s.cp('dve', s.oT[:, ob, q0:q0 + n], s.pd[OD][:, 0, 0:n], [('ps', 2 * OD)], [('o', ob, g['qtok'])])

        for st in range(n_it + 2):
            if st < n_it:
                A(items[st])
            if 0 <= st - 1 < n_it:
                B(items[st - 1])
            if 0 <= st - 2 < n_it:
                C(items[st - 2])
            if hook:
                hook(st)

    def sm_attn(s, groups, ob, bs, ksrc, vsrc, gmbuf=None, zrot=None, hook=None):
        OD = 2
        items = []
        for g in groups:
            nb = len(g['blocks'])
            for bi, (k0, ks, goff) in enumerate(g['blocks']):
                items.append(dict(g=g, k0=k0, ks=ks, goff=goff, first=(bi == 0), last=(bi == nb - 1)))
        n_it = len(items)
        pdt = lambda d: [('ps', 2 * d), ('ps', 2 * d + 1)]
        bpt = lambda d: [('bp', 2 * d), ('bp', 2 * d + 1)]

        def A(it):
            g = it['g']
            q0, n, ks, k0 = g['q0'], g['n'], it['ks'], it['k0']
            zd = (zrot or s.zdr).get()
            it['zd'] = zd
            has_g = it['goff'] is not None
            for hh in range(2):
                kap, ktok = ksrc(hh, k0, ks)
                s.mm(s.pd[zd][0:ks, hh, 0:n], kap, bs.q[64 * hh:64 * hh + 64, q0:q0 + n], True, not has_g,
                     ktok + [('q', bs.id, g['qtok'])] + bs.xt, [('ps', 2 * zd + hh)])
            if has_g:
                go = it['goff']
                for hh in range(2):
                    s.mm(s.pd[zd][0:ks, hh, 0:n], s.ident[0:ks, 0:ks], s.gm[0:ks, gmbuf[hh], go:go + n], False, True,
                         [('cst',), ('gm', gmbuf[hh])], [('ps', 2 * zd + hh)])
            w = s.bdr.get()
            it['w'] = w
            s.act(s.bp[0:ks, 2 * w:2 * w + 2, 0:n], s.pd[zd][0:ks, :, 0:n], AF.Exp, pdt(zd), bpt(w))

        def C(it):
            g = it['g']
            q0, n, ks, k0, w = g['q0'], g['n'], it['ks'], it['k0'], it['w']
            for hh in range(2):
                vap, vtok = vsrc(hh, k0, ks)
                s.mm(s.pd[OD][:, hh, 0:n], vap, s.bp[0:ks, 2 * w + hh, 0:n], it['first'], it['last'],
                     vtok + [('bp', 2 * w + hh)], [('ps', 2 * OD + hh)])
            if it['last']:
                for hh in range(2):
                    f = s.fpr.get()
                    dlo = 64 * (1 - hh)
                    s.act(s.fp[64 * hh:64 * hh + 64, f, 0:n], s.pd[OD][dlo:dlo + 64, hh, 0:n], AF.Ln,
                          [('ps', 2 * OD + hh)], [('fp', f)])
                    s.act(s.fp[64 * hh:64 * hh + 64, f, 0:n], s.fp[64 * hh:64 * hh + 64, f, 0:n], AF.Exp,
                          [('fp', f)], [('fp', f)], scale=-1.0)
                    s.tt('dve', s.oT[64 * hh:64 * hh + 64, ob, q0:q0 + n], s.pd[OD][64 * hh:64 * hh + 64, hh, 0:n],
                         s.fp[64 * hh:64 * hh + 64, f, 0:n], ALU.mult, [('ps', 2 * OD + hh), ('fp', f)],
                         [('o', ob, g['qtok'])])

        for st in range(n_it + 2):
            if st < n_it:
                A(items[st])
            if 0 <= st - 2 < n_it:
                C(items[st - 2])
            if hook:
                hook(st)

    def out_proj(s, tiles, l, j, ob, flush=None):
        c = s.cfg
        wb = s.woutr.get()
        s.load_w(('out', l, j), s.wout[:, wb, :], ('wout', wb))
        s.opq.append((wb, ob))
        if len(s.opq) < 2 and not flush:
            return
        q = s.opq
        s.opq = []
        s.opr = Rot([0, 1, 2, 3, 6, 7])
        k = 0
        for (ti, t0, n) in tiles:
            for dc in range(c.NCH):
                pd = s.opr.get()
                for qi, (wbb, obb) in enumerate(q):
                    s.mm(s.ps[pd][:, 0:n], s.wout[:, wbb, dc * 128:(dc + 1) * 128], s.oT[:, obb, t0:t0 + n],
                         qi == 0, qi == len(q) - 1, [('wout', wbb), ('o', obb, ti)], [('ps', pd)])
                s.resid(dc, ti, t0, n, pd, 1.0, 1 if k % 3 == 2 else 0)
                k += 1

    def mem_pair(s, tiles, l, jm, wkey, sample):
        c = s.cfg
        wb = s.winr.get()
        s.load_w(wkey, s.win[:, wb, 0:c.NCH * 128], ('win', wb))

        bs = s.bsets[0]

        def evq(ti, t0, n, pb):
            s.ts('dve', bs.q[:, t0:t0 + n], s.ps[pb][:, 0:n], 0.125, ALU.mult, [('ps', pb)], [('q', bs.id, ti)])

        s.proj_fm(tiles, wb, lambda ch: ch * 128, evq)
        ob = s.otr.get()
        groups = []
        for (ti, t0, n) in tiles:
            groups.append(dict(q0=t0, n=n, qtok=ti, blocks=[(k0, 128, None) for k0 in range(0, c.NMEM, 128)]))
        ksrc = lambda hh, k0, ks: (s.mk[64 * hh:64 * hh + 64, jm, k0:k0 + ks], [('mk',)])
        vsrc = lambda hh, k0, ks: (s.mv[0:ks, k0 // 128, jm * 256 + hh * 128: jm * 256 + (hh + 1) * 128], [('mv',)])
        s.sm_attn(groups, ob, bs, ksrc, vsrc)
        s.out_proj(tiles, l, c.NPA + jm, ob, flush=(jm == c.NPM - 1))

    def load_mem_kv(s, l, sample):
        c = s.cfg
        if sample:
            for jm in range(c.NPM):
                s.dma('pool', s.mk[:, jm, :], s.cmkT[l, jm * 128:(jm + 1) * 128, :], [], [('mk',)])
                for hh in range(2):
                    col = jm * 256 + (0 if hh == 0 else 192)
                    src = s.cmv[l, :, jm * 128 + hh * 64: jm * 128 + hh * 64 + 64].rearrange("(u p) d -> p u d", p=128)
                    s.dma('pool', s.mv[:, :, col:col + 64], src, [], [('mv',)])
        else:
            for jm in range(c.NPM):
                s.dma('sp', s.mk[:, jm, :], s.mk_sc[l, jm], [('mksc', l, jm)], [('mk',)])
                s.dma('sp', s.mv[:, :, jm * 256:(jm + 1) * 256], s.mv_sc[l, jm].rearrange("(u p) f -> p u f", p=128),
                      [('mvsc', l, jm)], [('mv',)])

    def mem_prep(s, sq):
        c = s.cfg
        NM = c.NMEM
        s.dma('sp', s.xT[:, :, 0:NM], s.memT_p[sq].rearrange("(c p) t -> p c t", p=128), [],
              [('x', ch, 0) for ch in range(c.NCH)])
        tiles = [(0, 0, NM)]
        r = s.norm_stats(tiles[0])
        for l in range(c.L):
            s.norm_apply(tiles[0], r, c.gcol('mem', l))
            for jm in range(c.NPM):
                wb = s.winr.get()
                s.load_w(('memkv', l, jm), s.win[:, wb, 0:c.NCH * 2 * 128], ('win', wb))

                def evk(ti, t0, n, pb, l=l, jm=jm):
                    f = s.fpr.get()
                    s.cp('dve', s.fp[:, f, 0:n], s.ps[pb][:, 0:n], [('ps', pb)], [('fp', f)])
                    s.dma('sp', s.mkT_p[l, sq, jm * 128:(jm + 1) * 128, :], s.fp[:, f, 0:n], [('fp', f)], [])
                    s.dma('pool', s.mk_sc[l, jm], s.fp[:, f, 0:n], [('fp', f)], [('mksc', l, jm)])

                s.proj_fm(tiles, wb, lambda ch: (ch * 2 + 0) * 128, evk)

                def evv(u0, nu, rows, pb, l=l, jm=jm):
                    f = s.fpr.get()
                    s.cp('dve', s.fp[:, f, 0:nu * 128], s.ps[pb][:, 0:nu * 128], [('ps', pb)], [('fp', f)])
                    src = s.fp[:, f, 0:nu * 128].rearrange("p (u d) -> p u d", d=128)
                    s.dma('sp', s.mv_p[l, sq, jm, u0 * 128:(u0 + nu) * 128, :].rearrange("(u p) d -> p u d", p=128),
                          src, [('fp', f)], [])
                    s.cp('pool', s.mv[:, u0:u0 + nu, jm * 256:jm * 256 + 64], src[:, :, 0:64], [('fp', f)], [('mv',)])
                    s.cp('pool', s.mv[:, u0:u0 + nu, jm * 256 + 192:jm * 256 + 256], src[:, :, 64:128], [('fp', f)],
                         [('mv',)])
                    s.dma('sp', s.mv_sc[l, jm, u0 * 128:(u0 + nu) * 128, :].rearrange("(u p) f -> p u f", p=128),
                          s.mv[:, u0:u0 + nu, jm * 256:(jm + 1) * 256], [('mv',)], [('mvsc', l, jm)])

                s.proj_tm(NM, wb, lambda ch: (ch * 2 + 1) * 128, evv)

    def set_aug(s, bs, val):
        c = s.cfg
        s.P.add('pool', lambda e: e.memset(bs.v[:, :, 64:192], val), [],
                [('v', bs.id, i) for i in range((c.NKT + 3) // 4)] + bs.xt)

    def pair_prep(s, tiles, l, j, sq, sample, bs, rot):
        c = s.cfg
        T = c.TS if sample else c.T
        QS = c.PAST if sample else 0
        isA = l < c.LA
        st = {}
        chunks = []
        bid = bs.id

        def c0():
            wb = s.winr.get()
            st['wb'] = wb
            if isA:
                s.load_w(('ina', l, j), s.win[:, wb, :], ('win', wb))
            else:
                s.load_w(('inb', l - c.LA, j), s.win[:, wb, 0:c.NCH * 128], ('win', wb))
        chunks.append(c0)
        qcol = (lambda ch: (ch * 3 + 0) * 128) if isA else (lambda ch: ch * 128)

        def evq(ti, t0, n, pb):
            s.ts('dve', bs.q[:, t0:t0 + n], s.ps[pb][:, 0:n], 0.125, ALU.mult, [('ps', pb)],
                 [('q', bid, ti)] + bs.xt)

        def late(fn_chunks):
            holder = {}

            def mk(i):
                def run():
                    if 'l' not in holder:
                        holder['l'] = fn_chunks()
                    holder['l'][i]()
                return run
            return mk

        nq = len(tiles)
        mkq = late(lambda: s.proj_fm_chunks(tiles, st['wb'], qcol, evq, rot))
        for i in range(nq):
            chunks.append(mkq(i))
        if isA:
            KO = c.PAST if sample else 0

            def evk(ti, t0, n, pb):
                f = s.fpr.get()
                s.cp('dve', s.fp[:, f, 0:n], s.ps[pb][:, 0:n], [('ps', pb)], [('fp', f)])
                dst = (s.akT_s[l, j * 128:(j + 1) * 128, t0:t0 + n] if sample
                       else s.akT_p[l, sq, j * 128:(j + 1) * 128, t0:t0 + n])
                s.dma('sp', dst, s.fp[:, f, 0:n], [('fp', f)], [])
                s.cp('pool', bs.k[:, KO + t0:KO + t0 + n], s.fp[:, f, 0:n], [('fp', f)],
                     [('k', bid, (KO + t0) // 512)] + bs.xt)

            mkk = late(lambda: s.proj_fm_chunks(tiles, st['wb'], lambda ch: (ch * 3 + 1) * 128, evk, rot))
            for i in range(nq):
                chunks.append(mkk(i))

            def evv(u0, nu, rows, pb):
                f = s.fpr.get()
                s.cp('dve', s.fp[0:rows, f, 0:nu * 128], s.ps[pb][0:rows, 0:nu * 128], [('ps', pb)], [('fp', f)])
                src = s.fp[0:rows, f, 0:nu * 128].rearrange("p (u d) -> p u d", d=128)
                if sample:
                    dst = s.av_s[l, j, 0:rows, :].rearrange("(u p) d -> p u d", p=rows)
                else:
                    dst = s.av_p[l, sq, j, u0 * 128:(u0 + nu) * 128, :].rearrange("(u p) d -> p u d", p=128)
                s.dma('sp', dst, src, [('fp', f)], [])
                kt0 = KO // 128 + u0
                vt = [('v', bid, kt // 4) for kt in sorted(set([kt0, kt0 + nu - 1]))] + bs.xt
                s.cp('pool', bs.v[0:rows, kt0:kt0 + nu, 0:64], src[:, :, 0:64], [('fp', f)], vt)
                s.cp('pool', bs.v[0:rows, kt0:kt0 + nu, 192:256], src[:, :, 64:128], [('fp', f)], vt)

            nv = ((T + 127) // 128 + 3) // 4
            mkv = late(lambda: s.proj_tm_chunks(T, st['wb'], lambda ch: (ch * 3 + 2) * 128, evv, rot))
            for i in range(nv):
                chunks.append(mkv(i))
            if sample:
                def cpast():
                    ktoks = [('k', bid, i) for i in range((c.PAST + 511) // 512)] + bs.xt
                    s.dma('pool', bs.k[:, 0:c.PAST], s.cakT[l, j * 128:(j + 1) * 128, :], [], ktoks)
                    vtoks = [('v', bid, i) for i in range((c.PAST // 128 + 3) // 4)] + bs.xt
                    for hh in range(2):
                        col = 0 if hh == 0 else 192
                        src = s.cav[l, :, j * 128 + hh * 64: j * 128 + hh * 64 + 64].rearrange("(u p) d -> p u d", p=128)
                        s.dma('pool', bs.v[:, 0:c.PAST // 128, col:col + 64], src, [], vtoks)
                chunks.insert(1, cpast)
            groups = []
            for (ti, t0, n) in tiles:
                qlo = QS + t0
                qhi = QS + t0 + n
                blocks = []
                KTOT = KO + T
                k0 = 0
                while k0 < min(KTOT, qhi):
                    ks = min(128, KTOT - k0)
                    diag = (k0 + ks - 1) >= qlo
                    moff = (XOFF + (qlo - k0)) if diag else None
                    qs = (max(0, k0 - qlo) // 64) * 64 if diag else 0
                    blocks.append((k0, ks, moff, qs))
                    k0 += 128
                blocks.reverse()
                groups.append(dict(q0=t0, n=n, qtok=ti, blocks=blocks))

            def attn(hook):
                ob = s.otr.get()
                s.sb_attn(groups, ob, bs, hook)
                return ob
        else:
            lb = l - c.LA
            if sample:
                nk = c.WBC + T
                KP0 = c.PAST - c.WBC

                def ckv():
                    s.dma('pool', bs.k[:, 0:c.WBC], s.cbkT[j * 128:(j + 1) * 128, :], [],
                          [('k', bid, i) for i in range((c.WBC + 511) // 512)] + bs.xt)
                    s.dma('sp', bs.k[:, c.WBC:c.WBC + T], s.kb_sc[c.NSEQ, j, :, 0:T], [('kbsc', c.NSEQ, j)],
                          [('k', bid, c.WBC // 512)] + bs.xt)
                    vtoks = [('v', bid, i) for i in range((c.WBC // 128 + 3) // 4)] + bs.xt
                    for hh in range(2):
                        col = 0 if hh == 0 else 192
                        src = s.cbv[:, j * 128 + hh * 64: j * 128 + hh * 64 + 64].rearrange("(u p) d -> p u d", p=128)
                        s.dma('pool', bs.v[:, 0:c.WBC // 128, col:col + 64], src, [], vtoks)
                    s.dma('sp', bs.v[0:T, c.WBC // 128, :], s.vb_sc[c.NSEQ, j, 0:T, :], [('vbsc', c.NSEQ, j)],
                          [('v', bid, (c.WBC // 128) // 4)] + bs.xt)
            else:
                nk = T
                KP0 = 0

                def ckv():
                    s.dma('sp', bs.k[:, 0:T], s.kb_sc[sq, j], [('kbsc', sq, j)],
                          [('k', bid, i) for i in range(T // 512)] + bs.xt)
                    s.dma('sp', bs.v[:, 0:T // 128, :], s.vb_sc[sq, j].rearrange("(u p) f -> p u f", p=128),
                          [('vbsc', sq, j)], [('v', bid, i) for i in range((T // 128 + 3) // 4)] + bs.xt)
            chunks.insert(1, ckv)
            groups = []
            for (ti, t0, n) in tiles:
                q0g = QS + t0
                lo = (q0g // 64 - 8) * 64
                hi = ((q0g + n - 1) // 64 + 1) * 64
                blocks = []
                k0 = 0
                while k0 < nk:
                    ks = min(128, nk - k0)
                    k0g = KP0 + k0
                    if k0g + ks > lo and k0g < hi:
                        dlt = q0g - k0g
                        assert -384 <= dlt <= 512 and dlt % 64 == 0, dlt
                        blocks.append((k0, ks, XOFF + dlt))
                    k0 += 128
                groups.append(dict(q0=t0, n=n, qtok=ti, blocks=blocks))
            ksrc = lambda hh, k0, ks: (bs.k[64 * hh:64 * hh + 64, k0:k0 + ks], [('k', bid, k0 // 512)] + bs.xt)
            vsrc = lambda hh, k0, ks: (bs.v[0:ks, k0 // 128, hh * 128:(hh + 1) * 128],
                                       [('v', bid, (k0 // 128) // 4)] + bs.xt)

            def attn(hook):
                ob = s.otr.get()
                gmb = {}
                for hh in range(2):
                    gb = s.gmr.get()
                    gmb[hh] = gb
                    s.dma('pool', s.gm[:, gb, :], s.gm_d[lb, 2 * j + hh], [], [('gm', gb)])
                s.sm_attn(groups, ob, bs, ksrc, vsrc, gmb, zrot=s.zdr2, hook=hook)
                return ob
        return chunks, attn

    def mixer(s, tiles, l, sq, sample):
        c = s.cfg
        isA = l < c.LA
        s.norm(tiles, c.gcol('mix', l))
        s.load_mem_kv(l, sample)
        aug = 0.0 if isA else 1.0
        for bs in s.bsets:
            s.set_aug(bs, aug)
        rot_ov = s.ovr_a if isA else s.mr
        chunks, attn = s.pair_prep(tiles, l, 0, sq, sample, s.bsets[1], s.mr)
        for ch in chunks:
            ch()
        for j in range(c.NPA):
            nxt = None
            if j + 1 < c.NPA:
                nchunks, nattn = s.pair_prep(tiles, l, j + 1, sq, sample, s.bsets[j % 2], rot_ov)
                pending = list(nchunks)
                stride = 2

                def hook(st, pending=pending):
                    if st % stride == 0 and pending:
                        pending.pop(0)()
            else:
                pending = []
                hook = None
            ob = attn(hook)
            while pending:
                pending.pop(0)()
            s.out_proj(tiles, l, j, ob)
            if j + 1 < c.NPA:
                attn = nattn
        for jm in range(c.NPM):
            wkey = ('inam', l, jm) if isA else ('inbm', l - c.LA, jm)
            s.mem_pair(tiles, l, jm, wkey, sample)

    def kvb(s, tiles, sq, sample):
        c = s.cfg
        T = c.TS if sample else c.T
        si = c.NSEQ if sample else sq
        s.norm(tiles, c.gcol('kv'))
        s.set_aug(s.bsets[0], 1.0)
        keep0 = 0 if sample else T - c.KEEP
        for j in range(c.NPA):
            wb = s.winr.get()
            s.load_w(('kvb', j), s.win[:, wb, 0:c.NCH * 2 * 128], ('win', wb))

            def evk(ti, t0, n, pb, j=j):
                f = s.fpr.get()
                s.cp('dve', s.fp[:, f, 0:n], s.ps[pb][:, 0:n], [('ps', pb)], [('fp', f)])
                if sample:
                    s.dma('sp', s.bkT_s[j * 128:(j + 1) * 128, t0:t0 + n], s.fp[:, f, 0:n], [('fp', f)], [])
                elif t0 + n > keep0:
                    a = max(t0, keep0)
                    s.dma('sp', s.bkT_p[sq, j * 128:(j + 1) * 128, a - keep0:t0 + n - keep0], s.fp[:, f, a - t0:n],
                          [('fp', f)], [])
                s.dma('pool', s.kb_sc[si, j, :, t0:t0 + n], s.fp[:, f, 0:n], [('fp', f)], [('kbsc', si, j)])

            s.proj_fm(tiles, wb, lambda ch: (ch * 2 + 0) * 128, evk)

            def evv(u0, nu, rows, pb, j=j):
                f = s.fpr.get()
                s.cp('dve', s.fp[0:rows, f, 0:nu * 128], s.ps[pb][0:rows, 0:nu * 128], [('ps', pb)], [('fp', f)])
                src = s.fp[0:rows, f, 0:nu * 128].rearrange("p (u d) -> p u d", d=128)
                if sample:
                    s.dma('sp', s.bv_s[j, 0:rows, :].rearrange("(u p) d -> p u d", p=rows), src, [('fp', f)], [])
                else:
                    for uu in range(nu):
                        tk0 = (u0 + uu) * 128
                        if tk0 >= keep0:
                            s.dma('sp', s.bv_p[sq, j, tk0 - keep0:tk0 - keep0 + 128, :], s.fp[:, f, uu * 128:(uu + 1) * 128],
                                  [('fp', f)], [])
                vt = [('v', 0, kt // 4) for kt in sorted(set([u0, u0 + nu - 1]))]
                s.cp('pool', s.vb[0:rows, u0:u0 + nu, 0:64], src[:, :, 0:64], [('fp', f)], vt)
                s.cp('pool', s.vb[0:rows, u0:u0 + nu, 192:256], src[:, :, 64:128], [('fp', f)], vt)
                if sample:
                    s.dma('sp', s.vb_sc[si, j, 0:rows, :], s.vb[0:rows, 0, :], vt, [('vbsc', si, j)])
                else:
                    s.dma('sp', s.vb_sc[si, j, u0 * 128:(u0 + nu) * 128, :].rearrange("(u p) f -> p u f", p=128),
                          s.vb[:, u0:u0 + nu, :], vt, [('vbsc', si, j)])

            s.proj_tm(T, wb, lambda ch: (ch * 2 + 1) * 128, evv)

    def run_seq(s, sq, sample):
        c = s.cfg
        T = c.TS if sample else c.T
        tiles = [(i, t0, min(512, T - t0)) for i, t0 in enumerate(range(0, T, 512))]
        if not sample:
            s.mem_prep(sq)
        src = s.xT_s if sample else s.xT_p[sq]
        xt = [('x', ch, ti) for ch in range(c.NCH) for (ti, _, _) in tiles]
        s.dma('sp', s.xT[:, :, 0:T], src.rearrange("(c p) t -> p c t", p=128), [], xt)
        for l in range(c.L):
            s.ffn(tiles, l, 1)
            s.mixer(tiles, l, sq, sample)
            s.ffn(tiles, l, 2)
            if l == c.LA - 1:
                s.kvb(tiles, sq, sample)
        gc = c.gcol('final')
        for tile in tiles:
            (ti, t0, n) = tile
            r = s.norm_stats(tile)
            for ch in range(c.NCH):
                f = s.fpr.get()
                s.stt('dve', s.fp[:, f, 0:n], s.xT[:, ch, t0:t0 + n], s.gains[:, gc + ch:gc + ch + 1],
                      s.rs[:, r, 0:n], ALU.mult, ALU.mult, [('x', ch, ti), ('rs', r), ('gains',)], [('fp', f)])
                dst = (s.yT_s[ch * 128:(ch + 1) * 128, t0:t0 + n] if sample
                       else s.yT_p[sq, ch * 128:(ch + 1) * 128, t0:t0 + n])
                s.dma('sp', dst, s.fp[:, f, 0:n], [('fp', f)], [])

    def build(s):
        s.prologue()
        for sq in range(s.cfg.NSEQ):
            s.run_seq(sq, False)
        s.run_seq(0, True)
        s.stats = s.P.emit(s.nc)
        return s.nc


def make_in_maps(cfg, inp):
    c = cfg
    nc_ = c.n_cores
    f = lambda a: np.ascontiguousarray(a, dtype=np.float32)
    wpack = pack_weights(c, inp)
    gains = pack_gains(c, inp)
    consts = make_consts(c)
    gm = make_gm(c, np.asarray(inp['rel_bias_b'], np.float32))
    xp = np.asarray(inp['x_prompt']).reshape(nc_, c.NSEQ, c.T, c.D)
    mp = np.asarray(inp['mem_prompt']).reshape(nc_, c.NSEQ, c.NMEM, c.D)
    maps = []
    for k in range(nc_):
        m = {
            'xT_p': f(xp[k].transpose(0, 2, 1)),
            'memT_p': f(mp[k].transpose(0, 2, 1)),
            'xT_s': f(np.asarray(inp['x_sample'])[k].T),
            'cakT': f(np.asarray(inp['cache_a_k'])[:, k].reshape(c.LA, c.PAST, c.AW).transpose(0, 2, 1)),
            'cav': f(np.asarray(inp['cache_a_v'])[:, k].reshape(c.LA, c.PAST, c.AW)),
            'cbkT': f(np.asarray(inp['cache_b_k'])[k].reshape(c.WBC, c.AW).T),
            'cbv': f(np.asarray(inp['cache_b_v'])[k].reshape(c.WBC, c.AW)),
            'cmkT': f(np.asarray(inp['cache_mem_k'])[:, k].reshape(c.L, c.NMEM, c.MW).transpose(0, 2, 1)),
            'cmv': f(np.asarray(inp['cache_mem_v'])[:, k].reshape(c.L, c.NMEM, c.MW)),
            'gains': gains, 'consts': consts, 'gm': gm, 'wpack': wpack,
        }
        maps.append(m)
    return maps


def assemble(cfg, results):
    c = cfg
    cat = lambda name: np.stack([np.asarray(r[name]) for r in results])
    B = c.n_cores * c.NSEQ
    y_prompt = cat('yT_p').transpose(0, 1, 3, 2).reshape(B, c.T, c.D)
    y_sample = cat('yT_s').transpose(0, 2, 1)
    ak = cat('akT_p').transpose(1, 0, 2, 4, 3).reshape(c.LA, B, c.T, c.HA, 64)
    av = cat('av_p').transpose(1, 0, 2, 4, 3, 5).reshape(c.LA, B, c.T, c.HA, 64)
    bk = cat('bkT_p').transpose(0, 1, 3, 2).reshape(B, c.KEEP, c.HA, 64)
    bv = cat('bv_p').transpose(0, 1, 3, 2, 4).reshape(B, c.KEEP, c.HA, 64)
    mk = cat('mkT_p').transpose(1, 0, 2, 4, 3).reshape(c.L, B, c.NMEM, c.HM, 64)
    mv = cat('mv_p').transpose(1, 0, 2, 4, 3, 5).reshape(c.L, B, c.NMEM, c.HM, 64)
    aks = cat('akT_s').transpose(1, 0, 3, 2).reshape(c.LA, c.n_cores, c.TS, c.HA, 64)
    avs = cat('av_s').transpose(1, 0, 3, 2, 4).reshape(c.LA, c.n_cores, c.TS, c.HA, 64)
    bks = cat('bkT_s').transpose(0, 2, 1).reshape(c.n_cores, c.TS, c.HA, 64)
    bvs = cat('bv_s').transpose(0, 2, 1, 3).reshape(c.n_cores, c.TS, c.HA, 64)
    outs = (y_prompt, y_sample, ak, av, bk, bv, mk, mv, aks, avs, bks, bvs)
    return tuple(np.ascontiguousarray(o, dtype=np.float32) for o in outs)


def run(cfg, inp, trace=False):
    b = Builder(cfg)
    nc = b.build()
    maps = make_in_maps(cfg, inp)
    res = run_bass_kernel_spmd(nc, maps, core_ids=list(range(cfg.n_cores)), trace=trace)
    return assemble(cfg, res.results), res, b


def kernel(**inputs):
    cfg = Cfg()
    outs, _, _ = run(cfg, inputs)
    return outs
```

```python
import numpy as np
import concourse.bass as bass
import concourse.mybir as mybir
from concourse.bass_utils import run_bass_kernel_spmd

F32 = mybir.dt.float32
BF16 = mybir.dt.bfloat16
AF = mybir.ActivationFunctionType
ALU = mybir.AluOpType
NEG = -30000.0
EPS = 1e-6
XOFF = 384
GMW = 1408
NMW = 896
WCH = 8192
LOOKAHEAD = 6


class Cfg:
    def __init__(s, D=1024, T=2048, NSEQ=4, DFF=2816, HA=12, HM=4, NMEM=256, PAST=1024, TS=32, L=4, G=2,
                 WB=512, n_cores=8):
        s.D, s.T, s.NSEQ, s.DFF, s.HA, s.HM, s.NMEM, s.PAST, s.TS, s.L, s.G, s.WB = D, T, NSEQ, DFF, HA, HM, NMEM, PAST, TS, L, G, WB
        s.n_cores = n_cores
        s.NCH = D // 128
        s.NF = DFF // 128
        assert s.NF % G == 0
        s.NG = s.NF // G
        s.NPA = HA // 2
        s.NPM = HM // 2
        s.LA = L // 2
        s.LB = L - s.LA
        s.AW = HA * 64
        s.MW = HM * 64
        assert (HA + HM) * 64 == D
        s.WBC = min(WB, PAST)
        s.KEEP = min(WB, T)
        s.KT = max(T, PAST + TS)
        s.NKT = (s.KT + 127) // 128
        s.NGC = (4 * L + 2) * s.NCH

    def gcol(s, name, l=0):
        order = {'ff1': 0, 'mix': 1, 'ff2': 2, 'mem': 3}
        if name == 'kv':
            return 4 * s.L * s.NCH
        if name == 'final':
            return (4 * s.L + 1) * s.NCH
        return (order[name] * s.L + l) * s.NCH


def weight_dir(cfg):
    blocks = []
    NCH, G, D = cfg.NCH, cfg.G, cfg.D
    for l in range(cfg.L):
        for jm in range(cfg.NPM):
            blocks.append((('memkv', l, jm), NCH * 2 * 128))
    for l in range(cfg.L):
        for g in range(cfg.NG):
            blocks.append((('gu', l, 1, g), NCH * 2 * G * 128))
            blocks.append((('dn', l, 1, g), G * D))
        if l < cfg.LA:
            for j in range(cfg.NPA):
                blocks.append((('ina', l, j), NCH * 3 * 128))
                blocks.append((('out', l, j), D))
            for jm in range(cfg.NPM):
                blocks.append((('inam', l, jm), NCH * 128))
                blocks.append((('out', l, cfg.NPA + jm), D))
        else:
            for j in range(cfg.NPA):
                blocks.append((('inb', l - cfg.LA, j), NCH * 128))
                blocks.append((('out', l, j), D))
            for jm in range(cfg.NPM):
                blocks.append((('inbm', l - cfg.LA, jm), NCH * 128))
                blocks.append((('out', l, cfg.NPA + jm), D))
        for g in range(cfg.NG):
            blocks.append((('gu', l, 2, g), NCH * 2 * G * 128))
            blocks.append((('dn', l, 2, g), G * D))
        if l == cfg.LA - 1:
            for j in range(cfg.NPA):
                blocks.append((('kvb', j), NCH * 2 * 128))
    d = {}
    off = 0
    for k, F in blocks:
        d[k] = (off, F)
        off += 128 * F
    chunk = 128 * WCH
    total = ((off + chunk - 1) // chunk) * chunk
    return blocks, d, total


def pack_weights(cfg, w):
    blocks, d, total = weight_dir(cfg)
    out = np.zeros(total, np.float32)
    NCH, G, D, DFF, AW = cfg.NCH, cfg.G, cfg.D, cfg.DFF, cfg.AW

    def kc(m):
        return m.reshape(NCH, 128, -1).transpose(1, 0, 2)

    for key, F in blocks:
        kind = key[0]
        if kind == 'gu':
            _, l, i, g = key
            m = w['w_ff1_gu' if i == 1 else 'w_ff2_gu'][l]
            f0 = g * G * 128
            blk = np.stack([kc(m[:, f0:f0 + G * 128]), kc(m[:, DFF + f0:DFF + f0 + G * 128])], axis=2)
        elif kind == 'dn':
            _, l, i, g = key
            m = w['w_ff1_down' if i == 1 else 'w_ff2_down'][l]
            blk = m[g * G * 128:(g + 1) * G * 128].reshape(G, 128, D).transpose(1, 0, 2)
        elif kind == 'ina':
            _, l, j = key
            m = w['w_in_a'][l]
            blk = np.stack([kc(m[:, s * AW + j * 128: s * AW + (j + 1) * 128]) for s in range(3)], axis=2)
        elif kind == 'inam':
            _, l, jm = key
            m = w['w_in_a'][l]
            blk = kc(m[:, 3 * AW + jm * 128: 3 * AW + (jm + 1) * 128])
        elif kind == 'inb':
            _, lb, j = key
            blk = kc(w['w_in_b'][lb][:, j * 128:(j + 1) * 128])
        elif kind == 'inbm':
            _, lb, jm = key
            blk = kc(w['w_in_b'][lb][:, AW + jm * 128: AW + (jm + 1) * 128])
        elif kind == 'out':
            _, l, j = key
            blk = w['w_out'][l][j * 128:(j + 1) * 128]
        elif kind == 'memkv':
            _, l, jm = key
            m = w['w_mem_kv'][l]
            blk = np.stack([kc(m[:, s * cfg.MW + jm * 128: s * cfg.MW + (jm + 1) * 128]) for s in range(2)], axis=2)
        elif kind == 'kvb':
            _, j = key
            m = w['w_kv_b']
            blk = np.stack([kc(m[:, s * AW + j * 128: s * AW + (j + 1) * 128]) for s in range(2)], axis=2)
        off = d[key][0]
        out[off:off + 128 * F] = np.ascontiguousarray(blk, dtype=np.float32).reshape(-1)
    return out


def pack_gains(cfg, w):
    cols = []
    for name in ('g_ff1', 'g_mix', 'g_ff2', 'g_mem'):
        for l in range(cfg.L):
            cols.append(w[name][l].reshape(cfg.NCH, 128).T)
    cols.append(w['g_kv'].reshape(cfg.NCH, 128).T)
    cols.append(w['g_final'].reshape(cfg.NCH, 128).T)
    return np.ascontiguousarray(np.concatenate(cols, axis=1), dtype=np.float32)


def make_consts(cfg):
    k = np.arange(128)[:, None]
    x = np.arange(NMW)[None, :] - XOFF
    nm = np.where(k >= x, NEG, 0.0).astype(np.float32)
    ident = np.eye(128, dtype=np.float32)
    j = np.arange(128)[:, None]
    s = np.arange(128)[None, :]
    negu = np.where(j >= s, -1.0, 0.0).astype(np.float32)
    negones = -np.ones((128, 128), np.float32)
    onesmean = np.full((128, 128), 1.0 / cfg.D, np.float32)
    return np.ascontiguousarray(np.concatenate([nm, ident, negu, negones, onesmean], axis=1))


def make_gm(cfg, rel_bias_b):
    k = np.arange(128)[:, None]
    x = np.arange(GMW)[None, :] - XOFF
    rel = np.clip(x - k, -128, 128) + 128
    xc = np.floor_divide(x, 64)
    kcn = k // 64
    vis = (kcn <= xc) & (xc <= kcn + 8)
    g = rel_bias_b[:, :, rel]
    return np.ascontiguousarray(np.where(vis[None, None], g, np.float32(NEG)), dtype=np.float32)


class Op:
    __slots__ = ('eng', 'fn', 'deps', 'dma', 'sig', 'sem', 'val', 'slot_prev')

    def __init__(s, eng, fn, dma):
        s.eng, s.fn, s.dma = eng, fn, dma
        s.deps = []
        s.sig = dma
        s.sem = None
        s.val = 0
        s.slot_prev = None


ENGS = ('pe', 'act', 'dve', 'pool', 'sp')


class Prog:
    def __init__(s):
        s.ops = []
        s.lw = {}
        s.rd = {}

    def add(s, eng, fn, reads=(), writes=(), dma=False):
        op = Op(eng, fn, dma)
        oid = len(s.ops)
        deps = set()
        for r in reads:
            w = s.lw.get(r)
            if w is not None:
                deps.add(w)
        for wt in writes:
            w = s.lw.get(wt)
            if w is not None:
                wo = s.ops[w]
                if wo.dma or dma or wo.eng != eng:
                    deps.add(w)
            rdd = s.rd.get(wt)
            if rdd:
                for (reng, rdma, _), rid in rdd.items():
                    if rdma or dma or reng != eng:
                        deps.add(rid)
        final = []
        for dd in deps:
            do = s.ops[dd]
            if (not do.dma) and (not dma) and do.eng == eng and eng == 'pe':
                continue
            final.append(dd)
            do.sig = True
        op.deps = final
        s.ops.append(op)
        for r in reads:
            dct = s.rd.setdefault(r, {})
            if dma:
                dct[(eng, True, oid)] = oid
            else:
                dct[(eng, False, 0)] = oid
        for wt in writes:
            s.lw[wt] = oid
            s.rd[wt] = {}
        return oid

    def emit(s, nc, n_dma_sems=None):
        n_dma_sems = n_dma_sems or {'sp': 24, 'pool': 48}
        csem = {e: nc.alloc_semaphore('s_' + e) for e in ('pe', 'act', 'dve', 'pool')}
        dsem = {q: [nc.alloc_semaphore('d_%s%d' % (q, i)) for i in range(n)] for q, n in n_dma_sems.items()}
        cnt = {e: 0 for e in csem}
        dcount = {q: 0 for q in dsem}
        slot_cnt = {q: [0] * len(dsem[q]) for q in dsem}
        for op in s.ops:
            if op.dma:
                q = op.eng
                i = dcount[q] % len(dsem[q])
                dcount[q] += 1
                op.slot_prev = slot_cnt[q][i]
                slot_cnt[q][i] += 16
                op.sem = dsem[q][i]
                op.val = slot_cnt[q][i]
            elif op.sig:
                cnt[op.eng] += 1
                op.sem = csem[op.eng]
                op.val = cnt[op.eng]
        per = {e: [] for e in ENGS}
        for op in s.ops:
            per[op.eng].append(op)
        ops = s.ops
        final_waits = []
        for q in dsem:
            for i, sm in enumerate(dsem[q]):
                if slot_cnt[q][i] > 0:
                    final_waits.append((sm, slot_cnt[q][i]))
        stats = {e: [len(per[e]), 0] for e in ENGS}

        def run(ename, e):
            waited = {}
            nw = 0
            for op in per[ename]:
                waits = {}
                for dd in op.deps:
                    do = ops[dd]
                    k = id(do.sem)
                    if k not in waits or waits[k][1] < do.val:
                        waits[k] = (do.sem, do.val)
                if op.dma and op.slot_prev:
                    k = id(op.sem)
                    if k not in waits or waits[k][1] < op.slot_prev:
                        waits[k] = (op.sem, op.slot_prev)
                for k, (sm, v) in waits.items():
                    if waited.get(k, 0) < v:
                        e.wait_ge(sm, v)
                        waited[k] = v
                        nw += 1
                ins = op.fn(e)
                if op.sig:
                    ins.then_inc(op.sem, 16 if op.dma else 1)
            if ename == 'sp':
                for sm, v in final_waits:
                    if waited.get(id(sm), 0) < v:
                        e.wait_ge(sm, v)
            stats[ename][1] = nw

        with nc.Block() as block:
            @block.tensor
            def _(e):
                run('pe', e)

            @block.scalar
            def _(e):
                run('act', e)

            @block.vector
            def _(e):
                run('dve', e)

            @block.gpsimd
            def _(e):
                run('pool', e)

            @block.sync
            def _(e):
                run('sp', e)
        return stats


class Rot:
    def __init__(s, items):
        s.items = list(items)
        s.i = 0

    def get(s):
        v = s.items[s.i % len(s.items)]
        s.i += 1
        return v


class Builder:
    def __init__(s, cfg):
        s.cfg = cfg
        s.P = Prog()
        s.nc = bass.Bass("TRN2", target_bir_lowering=False)
        s.blocks, s.wdir, s.wtotal = weight_dir(cfg)
        s._decl()

    def _decl(s):
        nc, c = s.nc, s.cfg
        di = lambda n, sh: nc.dram_tensor(n, sh, F32, kind="ExternalInput").ap()
        do = lambda n, sh: nc.dram_tensor(n, sh, F32, kind="ExternalOutput").ap()
        s.xT_p = di("xT_p", [c.NSEQ, c.D, c.T])
        s.memT_p = di("memT_p", [c.NSEQ, c.D, c.NMEM])
        s.xT_s = di("xT_s", [c.D, c.TS])
        s.cakT = di("cakT", [c.LA, c.AW, c.PAST])
        s.cav = di("cav", [c.LA, c.PAST, c.AW])
        s.cbkT = di("cbkT", [c.AW, c.WBC])
        s.cbv = di("cbv", [c.WBC, c.AW])
        s.cmkT = di("cmkT", [c.L, c.MW, c.NMEM])
        s.cmv = di("cmv", [c.L, c.NMEM, c.MW])
        s.gains_d = di("gains", [128, c.NGC])
        s.consts_d = di("consts", [128, GMW])
        s.gm_d = di("gm", [c.LB, c.HA, 128, GMW])
        s.wpack = di("wpack", [s.wtotal])
        s.yT_p = do("yT_p", [c.NSEQ, c.D, c.T])
        s.yT_s = do("yT_s", [c.D, c.TS])
        s.akT_p = do("akT_p", [c.LA, c.NSEQ, c.AW, c.T])
        s.av_p = do("av_p", [c.LA, c.NSEQ, c.NPA, c.T, 128])
        s.bkT_p = do("bkT_p", [c.NSEQ, c.AW, c.KEEP])
        s.bv_p = do("bv_p", [c.NSEQ, c.NPA, c.KEEP, 128])
        s.mkT_p = do("mkT_p", [c.L, c.NSEQ, c.MW, c.NMEM])
        s.mv_p = do("mv_p", [c.L, c.NSEQ, c.NPM, c.NMEM, 128])
        s.akT_s = do("akT_s", [c.LA, c.AW, c.TS])
        s.av_s = do("av_s", [c.LA, c.NPA, c.TS, 128])
        s.bkT_s = do("bkT_s", [c.AW, c.TS])
        s.bv_s = do("bv_s", [c.NPA, c.TS, 128])
        ds = lambda n, sh: nc.dram_tensor(n, sh, BF16, kind="Internal").ap()
        s.wsc = ds("wsc", [s.wtotal])
        s.kb_sc = ds("kb_sc", [c.NSEQ + 1, c.NPA, 128, c.T])
        s.vb_sc = ds("vb_sc", [c.NSEQ + 1, c.NPA, c.T, 256])
        s.mk_sc = ds("mk_sc", [c.L, c.NPM, 128, c.NMEM])
        s.mv_sc = ds("mv_sc", [c.L, c.NPM, c.NMEM, 256])
        sb = lambda n, sh, dt: nc.alloc_sbuf_tensor(n, sh, dt)
        s.xT = sb("xT", [128, c.NCH, c.T], F32)
        s.hT = sb("hT", [128, c.NCH, c.T], BF16)
        s.wgu = sb("wgu", [128, 2, c.NCH * 2 * c.G * 128], BF16)
        s.wd = sb("wd", [128, 2, c.G * c.D], BF16)
        s.win = sb("win", [128, 2, c.NCH * 3 * 128], BF16)
        s.wout = sb("wout", [128, 2, c.D], BF16)
        s.qb = sb("qb", [128, c.T], BF16)
        s.kb = sb("kb", [128, c.KT], BF16)
        s.vb = sb("vb", [128, c.NKT, 256], BF16)
        s.oT = sb("oT", [128, 2, c.T], BF16)
        s.NFP, s.NBP = 6, 12
        s.fp = sb("fp", [128, s.NFP, 512], F32)
        s.bp = sb("bp", [128, s.NBP, 512], BF16)
        s.rs = sb("rs", [128, 2, 512], F32)
        s.rsr = Rot([0, 1])
        s.gm = sb("gmb", [128, 2, GMW], BF16)
        s.cst = sb("cst", [128, GMW], BF16)
        s.gains = sb("gains_sb", [128, c.NGC], F32)
        s.mk = sb("mk", [128, c.NPM, c.NMEM], BF16)
        s.mv = sb("mv", [128, c.NMEM // 128, c.NPM * 256], BF16)
        s.pd = [nc.alloc_psum_tensor("pd%d" % i, [128, 2, 512], F32) for i in range(4)]
        s.ps = [s.pd[b // 2][:, b % 2, :] for b in range(8)]
        s.lac = sb("lac", [128, 2, 2, 512], BF16)
        s.lacr = Rot([0, 1])
        s.zdr = Rot([0, 1, 3])
        s.fdr = Rot(range(s.NFP // 2))
        s.bdr = Rot(range(s.NBP // 2))
        s.opq = []
        s.ovr_a = Rot([5])
        s.zdr2 = Rot([0, 1])
        s.opr = Rot([0, 1, 2, 3, 6, 7])
        s.fpr = Rot(range(s.NFP))
        s.bpr = Rot(range(s.NBP))
        s.zr = Rot([0, 1, 2, 3])
        s.orr = Rot([4, 5])
        s.mr = Rot([6, 7])
        s.gr = Rot([0, 1])
        s.ur = Rot([2, 3])
        s.dr = Rot([4, 5, 6, 7])
        s.wgur = Rot([0, 1])
        s.winr = Rot([0, 1])
        s.woutr = Rot([0, 1])
        s.otr = Rot([0, 1])
        s.gmr = Rot([0, 1])
        class BS:
            pass
        b0 = BS()
        b0.id, b0.q, b0.k, b0.v, b0.xt = 0, s.qb, s.kb, s.vb, []
        half = (c.NCH * 2 * c.G * 128) // 2
        assert c.T <= half and c.KT <= half and c.NKT * 256 <= 2 * half
        b1 = BS()
        b1.id = 1
        b1.q = s.wgu[:, 0, 0:half]
        b1.k = s.wgu[:, 0, half:2 * half]
        b1.v = s.wgu[:, 1, 0:c.NKT * 256].rearrange("p (k f) -> p k f", f=256)
        b1.xt = [('wgu', 0), ('wgu', 1)]
        s.bsets = [b0, b1]
        s.nm = s.cst[:, 0:NMW]
        s.ident = s.cst[:, NMW:NMW + 128]
        s.negu = s.cst[:, NMW + 128:NMW + 256]
        s.negones = s.cst[:, NMW + 256:NMW + 384]
        s.onesmean = s.cst[:, NMW + 384:NMW + 512]

    def mm(s, out, lhsT, rhs, start, stop, reads, writes):
        s.P.add('pe', lambda e: e.matmul(out, lhsT=lhsT, rhs=rhs, start=start, stop=stop, skip_group_check=True),
                reads, writes)

    def act(s, out, in_, func, reads, writes, scale=1.0, bias=0.0):
        s.P.add('act', lambda e: e.activation(out=out, in_=in_, func=func, bias=bias, scale=scale), reads, writes)

    def tt(s, eng, out, in0, in1, op, reads, writes):
        s.P.add(eng, lambda e: e.tensor_tensor(out=out, in0=in0, in1=in1, op=op), reads, writes)

    def ts(s, eng, out, in0, sc, op, reads, writes):
        s.P.add(eng, lambda e: e.tensor_scalar(out=out, in0=in0, scalar1=sc, scalar2=None, op0=op), reads, writes)

    def stt(s, eng, out, in0, sc, in1, op0, op1, reads, writes):
        s.P.add(eng, lambda e: e.scalar_tensor_tensor(out=out, in0=in0, scalar=sc, in1=in1, op0=op0, op1=op1),
                reads, writes)

    def cp(s, eng, out, in_, reads, writes):
        s.P.add(eng, lambda e: e.tensor_copy(out=out, in_=in_), reads, writes)

    def dma(s, q, out, in_, reads, writes):
        s.P.add(q, lambda e: e.dma_start(out=out, in_=in_), reads, writes, dma=True)

    def wblock(s, key):
        off, F = s.wdir[key]
        return s.wsc[off:off + 128 * F].rearrange("(p f) -> p f", p=128), F, off

    def wtok(s, off, F):
        ch = 128 * WCH
        return [('wc', k) for k in range(off // ch, (off + 128 * F - 1) // ch + 1)]

    def ensure_chunks(s, upto):
        nchunks = s.wtotal // (128 * WCH)
        src = s.wpack.rearrange("(k p f) -> k p f", p=128, f=WCH)
        dst = s.wsc.rearrange("(k p f) -> k p f", p=128, f=WCH)
        while s.chunk_issued <= min(upto, nchunks - 1):
            k = s.chunk_issued
            s.dma('pool', dst[k], src[k], [], [('wc', k)])
            s.chunk_issued += 1

    def load_w(s, key, dst, dtok):
        ap, F, off = s.wblock(key)
        toks = s.wtok(off, F)
        s.ensure_chunks(toks[-1][1] + LOOKAHEAD)
        s.dma('sp', dst, ap, toks, [dtok])

    def prologue(s):
        c = s.cfg
        s.dma('pool', s.cst[:], s.consts_d, [], [('cst',)])
        s.dma('sp', s.gains[:], s.gains_d, [], [('gains',)])
        s.chunk_issued = 0
        s.ensure_chunks(LOOKAHEAD)
        s.P.add('pool', lambda e: e.memset(s.vb[:, :, 64:192], 1.0), [], [('v', 0, i) for i in range((c.NKT + 3) // 4)])
        for jm in range(c.NPM):
            s.P.add('pool', lambda e, jm=jm: e.memset(s.mv[:, :, jm * 256 + 64: jm * 256 + 192], 1.0), [], [('mv',)])

    def norm_stats(s, tile, src_tok='x'):
        c = s.cfg
        (ti, t0, n) = tile
        pb = s.mr.get()
        for ch in range(c.NCH):
            b = s.bpr.get()
            s.act(s.bp[:, b, 0:n], s.xT[:, ch, t0:t0 + n], AF.Square, [(src_tok, ch, ti)], [('bp', b)])
            s.mm(s.ps[pb][:, 0:n], s.onesmean, s.bp[:, b, 0:n], ch == 0, ch == c.NCH - 1,
                 [('bp', b), ('cst',)], [('ps', pb)])
        f1 = s.fpr.get()
        s.act(s.fp[:, f1, 0:n], s.ps[pb][:, 0:n], AF.Ln, [('ps', pb)], [('fp', f1)], bias=EPS)
        r = s.rsr.get()
        s.act(s.rs[:, r, 0:n], s.fp[:, f1, 0:n], AF.Exp, [('fp', f1)], [('rs', r)], scale=-0.5)
        return r

    def norm_apply(s, tile, r, gc, htok='h'):
        c = s.cfg
        (ti, t0, n) = tile
        for ch in range(c.NCH):
            s.stt('dve', s.hT[:, ch, t0:t0 + n], s.xT[:, ch, t0:t0 + n], s.gains[:, gc + ch:gc + ch + 1],
                  s.rs[:, r, 0:n], ALU.mult, ALU.mult,
                  [('x', ch, ti), ('rs', r), ('gains',)], [(htok, ch, ti)])

    def norm(s, tiles, gc):
        for tile in tiles:
            r = s.norm_stats(tile)
            s.norm_apply(tile, r, gc)

    def resid(s, dc, ti, t0, n, pd, scale, path):
        if path == 0:
            if scale == 1.0:
                s.tt('dve', s.xT[:, dc, t0:t0 + n], s.ps[pd][:, 0:n], s.xT[:, dc, t0:t0 + n], ALU.add,
                     [('ps', pd), ('x', dc, ti)], [('x', dc, ti)])
            else:
                s.stt('dve', s.xT[:, dc, t0:t0 + n], s.ps[pd][:, 0:n], scale, s.xT[:, dc, t0:t0 + n], ALU.mult,
                      ALU.add, [('ps', pd), ('x', dc, ti)], [('x', dc, ti)])
        else:
            f = s.fpr.get()
            s.act(s.fp[:, f, 0:n], s.ps[pd][:, 0:n], AF.Copy, [('ps', pd)], [('fp', f)], scale=scale)
            s.tt('pool', s.xT[:, dc, t0:t0 + n], s.xT[:, dc, t0:t0 + n], s.fp[:, f, 0:n], ALU.add,
                 [('fp', f), ('x', dc, ti)], [('x', dc, ti)])

    def ffn(s, tiles, l, i):
        c = s.cfg
        G, NCH = c.G, c.NCH
        gc = c.gcol('ff1' if i == 1 else 'ff2', l)
        nt = len(tiles)

        def do_norm(k):
            if k < nt:
                r = s.norm_stats(tiles[k])
                s.norm_apply(tiles[k], r, gc)

        def load(g):
            b = s.wgur.get()
            s.load_w(('gu', l, i, g), s.wgu[:, b, :], ('wgu', b))
            s.load_w(('dn', l, i, g), s.wd[:, b, :], ('wd', b))
            return b

        def GU(b, tile):
            (ti, t0, n) = tile
            acts = []
            for fi in range(G):
                pg, pu = s.gr.get(), s.ur.get()
                for ch in range(NCH):
                    base = (ch * 2 + 0) * G * 128 + fi * 128
                    s.mm(s.ps[pg][:, 0:n], s.wgu[:, b, base:base + 128], s.hT[:, ch, t0:t0 + n], ch == 0,
                         ch == NCH - 1, [('wgu', b), ('h', ch, ti)], [('ps', pg)])
                for ch in range(NCH):
                    base = (ch * 2 + 1) * G * 128 + fi * 128
                    s.mm(s.ps[pu][:, 0:n], s.wgu[:, b, base:base + 128], s.hT[:, ch, t0:t0 + n], ch == 0,
                         ch == NCH - 1, [('wgu', b), ('h', ch, ti)], [('ps', pu)])
                f = s.fpr.get()
                s.act(s.fp[:, f, 0:n], s.ps[pg][:, 0:n], AF.Silu, [('ps', pg)], [('fp', f)])
                a = s.bpr.get()
                s.tt('dve', s.bp[:, a, 0:n], s.ps[pu][:, 0:n], s.fp[:, f, 0:n], ALU.mult,
                     [('ps', pu), ('fp', f)], [('bp', a)])
                acts.append(a)
            return acts

        def DOWN(b, tile, acts):
            (ti, t0, n) = tile
            for dc in range(NCH):
                pd = s.dr.get()
                for fi in range(G):
                    s.mm(s.ps[pd][:, 0:n], s.wd[:, b, fi * c.D + dc * 128: fi * c.D + (dc + 1) * 128],
                         s.bp[:, acts[fi], 0:n], fi == 0, fi == G - 1, [('wd', b), ('bp', acts[fi])], [('ps', pd)])
                s.resid(dc, ti, t0, n, pd, 0.5, dc % 2)

        do_norm(0)
        do_norm(1)
        bufs = {0: load(0)}
        steps = [(g, k) for g in range(c.NG) for k in range(nt)]
        prev = None
        for (g, k) in steps:
            acts = GU(bufs[g], tiles[k])
            if prev is not None:
                DOWN(*prev)
            prev = (bufs[g], tiles[k], acts)
            if k == 0 and g + 1 < c.NG:
                bufs[g + 1] = load(g + 1)
            if g == 0:
                do_norm(k + 2)
        DOWN(*prev)

    def proj_fm(s, tiles, wbuf, colf, evac, rot=None):
        for ch in s.proj_fm_chunks(tiles, wbuf, colf, evac, rot):
            ch()

    def proj_fm_chunks(s, tiles, wbuf, colf, evac, rot=None):
        c = s.cfg
        rot = rot or s.mr
        out = []
        for (ti, t0, n) in tiles:
            def chunk(ti=ti, t0=t0, n=n):
                pb = rot.get()
                for ch in range(c.NCH):
                    cb = colf(ch)
                    s.mm(s.ps[pb][:, 0:n], s.win[:, wbuf, cb:cb + 128], s.hT[:, ch, t0:t0 + n], ch == 0,
                         ch == c.NCH - 1, [('win', wbuf), ('h', ch, ti)], [('ps', pb)])
                evac(ti, t0, n, pb)
            out.append(chunk)
        return out

    def proj_tm(s, T, wbuf, colf, evac, rot=None):
        for ch in s.proj_tm_chunks(T, wbuf, colf, evac, rot):
            ch()

    def proj_tm_chunks(s, T, wbuf, colf, evac, rot=None):
        c = s.cfg
        rot = rot or s.mr
        ntt = (T + 127) // 128
        out = []
        for u0 in range(0, ntt, 4):
            def chunk(u0=u0):
                nu = min(4, ntt - u0)
                pb = rot.get()
                rows = min(128, T - u0 * 128)
                for uu in range(nu):
                    tk0 = (u0 + uu) * 128
                    r = min(128, T - tk0)
                    for ch in range(c.NCH):
                        cb = colf(ch)
                        s.mm(s.ps[pb][0:r, uu * 128:(uu + 1) * 128], s.hT[:, ch, tk0:tk0 + r],
                             s.win[:, wbuf, cb:cb + 128], ch == 0, ch == c.NCH - 1,
                             [('win', wbuf), ('h', ch, tk0 // 512)], [('ps', pb)])
                evac(u0, nu, rows, pb)
            out.append(chunk)
        return out

    def sb_attn(s, groups, ob, bs, hook=None):
        OD = 2
        items = []
        for g in groups:
            nb = len(g['blocks'])
            g['lac'] = s.lacr.get()
            for bi, (k0, ks, moff, qs) in enumerate(g['blocks']):
                items.append(dict(g=g, k0=k0, ks=ks, moff=moff, qs=qs, first=(bi == 0), last=(bi == nb - 1)))
        n_it = len(items)
        pdt = lambda d: [('ps', 2 * d), ('ps', 2 * d + 1)]
        bpt = lambda d: [('bp', 2 * d), ('bp', 2 * d + 1)]
        fpt = lambda d: [('fp', 2 * d), ('fp', 2 * d + 1)]

        def A(it):
            g = it['g']
            q0, n, ks, k0, qs = g['q0'], g['n'], it['ks'], it['k0'], it['qs']
            zd = s.zdr.get()
            it['zd'] = zd
            if it['first']:
                lc = g['lac']
                s.P.add('pool', lambda e: e.memset(s.lac[:, lc, :, 0:n], 0.0), [], [('lac', lc)])
            for hh in range(2):
                s.mm(s.pd[zd][0:ks, hh, qs:n], bs.k[64 * hh:64 * hh + 64, k0:k0 + ks],
                     bs.q[64 * hh:64 * hh + 64, q0 + qs:q0 + n], True, False,
                     [('k', bs.id, k0 // 512), ('q', bs.id, g['qtok'])] + bs.xt, [('ps', 2 * zd + hh)])
            if it['moff'] is not None:
                mo = it['moff']
                for hh in range(2):
                    s.mm(s.pd[zd][0:ks, hh, qs:n], s.ident[0:ks, 0:ks], s.nm[0:ks, mo + qs:mo + n], False, False,
                         [('cst',)], [('ps', 2 * zd + hh)])
            e = s.fdr.get()
            s.act(s.fp[0:ks, 2 * e:2 * e + 2, qs:n], s.pd[zd][0:ks, :, qs:n], AF.Exp, pdt(zd), fpt(e))
            sp = s.bdr.get()
            it['sp'] = sp
            s.act(s.bp[0:ks, 2 * sp:2 * sp + 2, qs:n], s.fp[0:ks, 2 * e:2 * e + 2, qs:n], AF.Ln, fpt(e), bpt(sp),
                  bias=1.0)

        def B(it):
            g = it['g']
            n, ks, zd, sp, qs = g['n'], it['ks'], it['zd'], it['sp'], it['qs']
            first = it['first']
            lc = g['lac']
            for hh in range(2):
                s.mm(s.pd[zd][0:ks, hh, qs:n], s.negu[0:ks, 0:ks], s.bp[0:ks, 2 * sp + hh, qs:n], False, first,
                     [('bp', 2 * sp + hh), ('cst',)], [('ps', 2 * zd + hh)])
            if not first:
                for hh in range(2):
                    s.mm(s.pd[zd][0:ks, hh, qs:n], s.negones[:, 0:ks], s.lac[:, lc, hh, qs:n], False, True,
                         [('lac', lc), ('cst',)], [('ps', 2 * zd + hh)])
            if not it['last']:
                s.tt('pool', s.lac[0:ks, lc, :, qs:n], s.lac[0:ks, lc, :, qs:n], s.bp[0:ks, 2 * sp:2 * sp + 2, qs:n],
                     ALU.add, [('lac', lc)] + bpt(sp), [('lac', lc)])
            w = s.bdr.get()
            it['w'] = w
            s.act(s.bp[0:ks, 2 * w:2 * w + 2, qs:n], s.pd[zd][0:ks, :, qs:n], AF.Exp, pdt(zd), bpt(w))

        def C(it):
            g = it['g']
            q0, n, ks, k0, w, qs = g['q0'], g['n'], it['ks'], it['k0'], it['w'], it['qs']
            kt = k0 // 128
            for hh in range(2):
                s.mm(s.pd[OD][:, 0, qs:n], bs.v[0:ks, kt, hh * 128:(hh + 1) * 128], s.bp[0:ks, 2 * w + hh, qs:n],
                     it['first'] and hh == 0, it['last'] and hh == 1,
                     [('v', bs.id, kt // 4), ('bp', 2 * w + hh)] + bs.xt, [('ps', 2 * OD)])
            if it['last']:
                s.cp('dve', s.oT[:, ob, q0:q0 + n], s.pd[OD][:, 0, 0:n], [('ps', 2 * OD)], [('o', ob, g['qtok'])])

        for st in range(n_it + 2):
            if st < n_it:
                A(items[st])
            if 0 <= st - 1 < n_it:
                B(items[st - 1])
            if 0 <= st - 2 < n_it:
                C(items[st - 2])
            if hook:
                hook(st)

    def sm_attn(s, groups, ob, bs, ksrc, vsrc, gmbuf=None, zrot=None, hook=None):
        OD = 2
        items = []
        for g in groups:
            nb = len(g['blocks'])
            for bi, (k0, ks, goff) in enumerate(g['blocks']):
                items.append(dict(g=g, k0=k0, ks=ks, goff=goff, first=(bi == 0), last=(bi == nb - 1)))
        n_it = len(items)
        pdt = lambda d: [('ps', 2 * d), ('ps', 2 * d + 1)]
        bpt = lambda d: [('bp', 2 * d), ('bp', 2 * d + 1)]

        def A(it):
            g = it['g']
            q0, n, ks, k0 = g['q0'], g['n'], it['ks'], it['k0']
            zd = (zrot or s.zdr).get()
            it['zd'] = zd
            has_g = it['goff'] is not None
            for hh in range(2):
                kap, ktok = ksrc(hh, k0, ks)
                s.mm(s.pd[zd][0:ks, hh, 0:n], kap, bs.q[64 * hh:64 * hh + 64, q0:q0 + n], True, not has_g,
                     ktok + [('q', bs.id, g['qtok'])] + bs.xt, [('ps', 2 * zd + hh)])
            if has_g:
                go = it['goff']
                for hh in range(2):
                    s.mm(s.pd[zd][0:ks, hh, 0:n], s.ident[0:ks, 0:ks], s.gm[0:ks, gmbuf[hh], go:go + n], False, True,
                         [('cst',), ('gm', gmbuf[hh])], [('ps', 2 * zd + hh)])
            w = s.bdr.get()
            it['w'] = w
            s.act(s.bp[0:ks, 2 * w:2 * w + 2, 0:n], s.pd[zd][0:ks, :, 0:n], AF.Exp, pdt(zd), bpt(w))

        def C(it):
            g = it['g']
            q0, n, ks, k0, w = g['q0'], g['n'], it['ks'], it['k0'], it['w']
            for hh in range(2):
                vap, vtok = vsrc(hh, k0, ks)
                s.mm(s.pd[OD][:, hh, 0:n], vap, s.bp[0:ks, 2 * w + hh, 0:n], it['first'], it['last'],
                     vtok + [('bp', 2 * w + hh)], [('ps', 2 * OD + hh)])
            if it['last']:
                for hh in range(2):
                    f = s.fpr.get()
                    dlo = 64 * (1 - hh)
                    s.act(s.fp[64 * hh:64 * hh + 64, f, 0:n], s.pd[OD][dlo:dlo + 64, hh, 0:n], AF.Ln,
                          [('ps', 2 * OD + hh)], [('fp', f)])
                    s.act(s.fp[64 * hh:64 * hh + 64, f, 0:n], s.fp[64 * hh:64 * hh + 64, f, 0:n], AF.Exp,
                          [('fp', f)], [('fp', f)], scale=-1.0)
                    s.tt('dve', s.oT[64 * hh:64 * hh + 64, ob, q0:q0 + n], s.pd[OD][64 * hh:64 * hh + 64, hh, 0:n],
                         s.fp[64 * hh:64 * hh + 64, f, 0:n], ALU.mult, [('ps', 2 * OD + hh), ('fp', f)],
                         [('o', ob, g['qtok'])])

        for st in range(n_it + 2):
            if st < n_it:
                A(items[st])
            if 0 <= st - 2 < n_it:
                C(items[st - 2])
            if hook:
                hook(st)

    def out_proj(s, tiles, l, j, ob, flush=None):
        c = s.cfg
        wb = s.woutr.get()
        s.load_w(('out', l, j), s.wout[:, wb, :], ('wout', wb))
        s.opq.append((wb, ob))
        if len(s.opq) < 2 and not flush:
            return
        q = s.opq
        s.opq = []
        s.opr = Rot([0, 1, 2, 3, 6, 7])
        k = 0
        for (ti, t0, n) in tiles:
            for dc in range(c.NCH):
                pd = s.opr.get()
                for qi, (wbb, obb) in enumerate(q):
                    s.mm(s.ps[pd][:, 0:n], s.wout[:, wbb, dc * 128:(dc + 1) * 128], s.oT[:, obb, t0:t0 + n],
                         qi == 0, qi == len(q) - 1, [('wout', wbb), ('o', obb, ti)], [('ps', pd)])
                s.resid(dc, ti, t0, n, pd, 1.0, 1 if k % 3 == 2 else 0)
                k += 1

    def mem_pair(s, tiles, l, jm, wkey, sample):
        c = s.cfg
        wb = s.winr.get()
        s.load_w(wkey, s.win[:, wb, 0:c.NCH * 128], ('win', wb))

        bs = s.bsets[0]

        def evq(ti, t0, n, pb):
            s.ts('dve', bs.q[:, t0:t0 + n], s.ps[pb][:, 0:n], 0.125, ALU.mult, [('ps', pb)], [('q', bs.id, ti)])

        s.proj_fm(tiles, wb, lambda ch: ch * 128, evq)
        ob = s.otr.get()
        groups = []
        for (ti, t0, n) in tiles:
            groups.append(dict(q0=t0, n=n, qtok=ti, blocks=[(k0, 128, None) for k0 in range(0, c.NMEM, 128)]))
        ksrc = lambda hh, k0, ks: (s.mk[64 * hh:64 * hh + 64, jm, k0:k0 + ks], [('mk',)])
        vsrc = lambda hh, k0, ks: (s.mv[0:ks, k0 // 128, jm * 256 + hh * 128: jm * 256 + (hh + 1) * 128], [('mv',)])
        s.sm_attn(groups, ob, bs, ksrc, vsrc)
        s.out_proj(tiles, l, c.NPA + jm, ob, flush=(jm == c.NPM - 1))

    def load_mem_kv(s, l, sample):
        c = s.cfg
        if sample:
            for jm in range(c.NPM):
                s.dma('pool', s.mk[:, jm, :], s.cmkT[l, jm * 128:(jm + 1) * 128, :], [], [('mk',)])
                for hh in range(2):
                    col = jm * 256 + (0 if hh == 0 else 192)
                    src = s.cmv[l, :, jm * 128 + hh * 64: jm * 128 + hh * 64 + 64].rearrange("(u p) d -> p u d", p=128)
                    s.dma('pool', s.mv[:, :, col:col + 64], src, [], [('mv',)])
        else:
            for jm in range(c.NPM):
                s.dma('sp', s.mk[:, jm, :], s.mk_sc[l, jm], [('mksc', l, jm)], [('mk',)])
                s.dma('sp', s.mv[:, :, jm * 256:(jm + 1) * 256], s.mv_sc[l, jm].rearrange("(u p) f -> p u f", p=128),
                      [('mvsc', l, jm)], [('mv',)])

    def mem_prep(s, sq):
        c = s.cfg
        NM = c.NMEM
        s.dma('sp', s.xT[:, :, 0:NM], s.memT_p[sq].rearrange("(c p) t -> p c t", p=128), [],
              [('x', ch, 0) for ch in range(c.NCH)])
        tiles = [(0, 0, NM)]
        r = s.norm_stats(tiles[0])
        for l in range(c.L):
            s.norm_apply(tiles[0], r, c.gcol('mem', l))
            for jm in range(c.NPM):
                wb = s.winr.get()
                s.load_w(('memkv', l, jm), s.win[:, wb, 0:c.NCH * 2 * 128], ('win', wb))

                def evk(ti, t0, n, pb, l=l, jm=jm):
                    f = s.fpr.get()
                    s.cp('dve', s.fp[:, f, 0:n], s.ps[pb][:, 0:n], [('ps', pb)], [('fp', f)])
                    s.dma('sp', s.mkT_p[l, sq, jm * 128:(jm + 1) * 128, :], s.fp[:, f, 0:n], [('fp', f)], [])
                    s.dma('pool', s.mk_sc[l, jm], s.fp[:, f, 0:n], [('fp', f)], [('mksc', l, jm)])

                s.proj_fm(tiles, wb, lambda ch: (ch * 2 + 0) * 128, evk)

                def evv(u0, nu, rows, pb, l=l, jm=jm):
                    f = s.fpr.get()
                    s.cp('dve', s.fp[:, f, 0:nu * 128], s.ps[pb][:, 0:nu * 128], [('ps', pb)], [('fp', f)])
                    src = s.fp[:, f, 0:nu * 128].rearrange("p (u d) -> p u d", d=128)
                    s.dma('sp', s.mv_p[l, sq, jm, u0 * 128:(u0 + nu) * 128, :].rearrange("(u p) d -> p u d", p=128),
                          src, [('fp', f)], [])
                    s.cp('pool', s.mv[:, u0:u0 + nu, jm * 256:jm * 256 + 64], src[:, :, 0:64], [('fp', f)], [('mv',)])
                    s.cp('pool', s.mv[:, u0:u0 + nu, jm * 256 + 192:jm * 256 + 256], src[:, :, 64:128], [('fp', f)],
                         [('mv',)])
                    s.dma('sp', s.mv_sc[l, jm, u0 * 128:(u0 + nu) * 128, :].rearrange("(u p) f -> p u f", p=128),
                          s.mv[:, u0:u0 + nu, jm * 256:(jm + 1) * 256], [('mv',)], [('mvsc', l, jm)])

                s.proj_tm(NM, wb, lambda ch: (ch * 2 + 1) * 128, evv)

    def set_aug(s, bs, val):
        c = s.cfg
        s.P.add('pool', lambda e: e.memset(bs.v[:, :, 64:192], val), [],
                [('v', bs.id, i) for i in range((c.NKT + 3) // 4)] + bs.xt)

    def pair_prep(s, tiles, l, j, sq, sample, bs, rot):
        c = s.cfg
        T = c.TS if sample else c.T
        QS = c.PAST if sample else 0
        isA = l < c.LA
        st = {}
        chunks = []
        bid = bs.id

        def c0():
            wb = s.winr.get()
            st['wb'] = wb
            if isA:
                s.load_w(('ina', l, j), s.win[:, wb, :], ('win', wb))
            else:
                s.load_w(('inb', l - c.LA, j), s.win[:, wb, 0:c.NCH * 128], ('win', wb))
        chunks.append(c0)
        qcol = (lambda ch: (ch * 3 + 0) * 128) if isA else (lambda ch: ch * 128)

        def evq(ti, t0, n, pb):
            s.ts('dve', bs.q[:, t0:t0 + n], s.ps[pb][:, 0:n], 0.125, ALU.mult, [('ps', pb)],
                 [('q', bid, ti)] + bs.xt)

        def late(fn_chunks):
            holder = {}

            def mk(i):
                def run():
                    if 'l' not in holder:
                        holder['l'] = fn_chunks()
                    holder['l'][i]()
                return run
            return mk

        nq = len(tiles)
        mkq = late(lambda: s.proj_fm_chunks(tiles, st['wb'], qcol, evq, rot))
        for i in range(nq):
            chunks.append(mkq(i))
        if isA:
            KO = c.PAST if sample else 0

            def evk(ti, t0, n, pb):
                f = s.fpr.get()
                s.cp('dve', s.fp[:, f, 0:n], s.ps[pb][:, 0:n], [('ps', pb)], [('fp', f)])
                dst = (s.akT_s[l, j * 128:(j + 1) * 128, t0:t0 + n] if sample
                       else s.akT_p[l, sq, j * 128:(j + 1) * 128, t0:t0 + n])
                s.dma('sp', dst, s.fp[:, f, 0:n], [('fp', f)], [])
                s.cp('dve', bs.k[:, KO + t0:KO + t0 + n], s.ps[pb][:, 0:n], [('ps', pb)],
                     [('k', bid, (KO + t0) // 512)] + bs.xt)

            mkk = late(lambda: s.proj_fm_chunks(tiles, st['wb'], lambda ch: (ch * 3 + 1) * 128, evk, rot))
            for i in range(nq):
                chunks.append(mkk(i))

            def evv(u0, nu, rows, pb):
                f = s.fpr.get()
                s.cp('dve', s.fp[0:rows, f, 0:nu * 128], s.ps[pb][0:rows, 0:nu * 128], [('ps', pb)], [('fp', f)])
                src = s.fp[0:rows, f, 0:nu * 128].rearrange("p (u d) -> p u d", d=128)
                if sample:
                    dst = s.av_s[l, j, 0:rows, :].rearrange("(u p) d -> p u d", p=rows)
                else:
                    dst = s.av_p[l, sq, j, u0 * 128:(u0 + nu) * 128, :].rearrange("(u p) d -> p u d", p=128)
                s.dma('sp', dst, src, [('fp', f)], [])
                kt0 = KO // 128 + u0
                vt = [('v', bid, kt // 4) for kt in sorted(set([kt0, kt0 + nu - 1]))] + bs.xt
                psrc = s.ps[pb][0:rows, 0:nu * 128].rearrange("p (u d) -> p u d", d=128)
                s.cp('dve', bs.v[0:rows, kt0:kt0 + nu, 0:64], psrc[:, :, 0:64], [('ps', pb)], vt)
                s.cp('dve', bs.v[0:rows, kt0:kt0 + nu, 192:256], psrc[:, :, 64:128], [('ps', pb)], vt)

            nv = ((T + 127) // 128 + 3) // 4
            mkv = late(lambda: s.proj_tm_chunks(T, st['wb'], lambda ch: (ch * 3 + 2) * 128, evv, rot))
            for i in range(nv):
                chunks.append(mkv(i))
            if sample:
                def cpast():
                    ktoks = [('k', bid, i) for i in range((c.PAST + 511) // 512)] + bs.xt
                    s.dma('pool', bs.k[:, 0:c.PAST], s.cakT[l, j * 128:(j + 1) * 128, :], [], ktoks)
                    vtoks = [('v', bid, i) for i in range((c.PAST // 128 + 3) // 4)] + bs.xt
                    for hh in range(2):
                        col = 0 if hh == 0 else 192
                        src = s.cav[l, :, j * 128 + hh * 64: j * 128 + hh * 64 + 64].rearrange("(u p) d -> p u d", p=128)
                        s.dma('pool', bs.v[:, 0:c.PAST // 128, col:col + 64], src, [], vtoks)
                chunks.insert(1, cpast)
            groups = []
            for (ti, t0, n) in tiles:
                qlo = QS + t0
                qhi = QS + t0 + n
                blocks = []
                KTOT = KO + T
                k0 = 0
                while k0 < min(KTOT, qhi):
                    ks = min(128, KTOT - k0)
                    diag = (k0 + ks - 1) >= qlo
                    moff = (XOFF + (qlo - k0)) if diag else None
                    qs = (max(0, k0 - qlo) // 64) * 64 if diag else 0
                    blocks.append((k0, ks, moff, qs))
                    k0 += 128
                blocks.reverse()
                groups.append(dict(q0=t0, n=n, qtok=ti, blocks=blocks))

            def attn(hook):
                ob = s.otr.get()
                s.sb_attn(groups, ob, bs, hook)
                return ob
            attn.nsteps = sum(len(g_['blocks']) for g_ in groups) + 2
        else:
            lb = l - c.LA
            if sample:
                nk = c.WBC + T
                KP0 = c.PAST - c.WBC

                def ckv():
                    s.dma('pool', bs.k[:, 0:c.WBC], s.cbkT[j * 128:(j + 1) * 128, :], [],
                          [('k', bid, i) for i in range((c.WBC + 511) // 512)] + bs.xt)
                    s.dma('sp', bs.k[:, c.WBC:c.WBC + T], s.kb_sc[c.NSEQ, j, :, 0:T], [('kbsc', c.NSEQ, j)],
                          [('k', bid, c.WBC // 512)] + bs.xt)
                    vtoks = [('v', bid, i) for i in range((c.WBC // 128 + 3) // 4)] + bs.xt
                    for hh in range(2):
                        col = 0 if hh == 0 else 192
                        src = s.cbv[:, j * 128 + hh * 64: j * 128 + hh * 64 + 64].rearrange("(u p) d -> p u d", p=128)
                        s.dma('pool', bs.v[:, 0:c.WBC // 128, col:col + 64], src, [], vtoks)
                    s.dma('sp', bs.v[0:T, c.WBC // 128, :], s.vb_sc[c.NSEQ, j, 0:T, :], [('vbsc', c.NSEQ, j)],
                          [('v', bid, (c.WBC // 128) // 4)] + bs.xt)
            else:
                nk = T
                KP0 = 0

                def ckv():
                    s.dma('sp', bs.k[:, 0:T], s.kb_sc[sq, j], [('kbsc', sq, j)],
                          [('k', bid, i) for i in range(T // 512)] + bs.xt)
                    s.dma('sp', bs.v[:, 0:T // 128, :], s.vb_sc[sq, j].rearrange("(u p) f -> p u f", p=128),
                          [('vbsc', sq, j)], [('v', bid, i) for i in range((T // 128 + 3) // 4)] + bs.xt)
            chunks.insert(1, ckv)
            groups = []
            for (ti, t0, n) in tiles:
                q0g = QS + t0
                lo = (q0g // 64 - 8) * 64
                hi = ((q0g + n - 1) // 64 + 1) * 64
                blocks = []
                k0 = 0
                while k0 < nk:
                    ks = min(128, nk - k0)
                    k0g = KP0 + k0
                    if k0g + ks > lo and k0g < hi:
                        dlt = q0g - k0g
                        assert -384 <= dlt <= 512 and dlt % 64 == 0, dlt
                        blocks.append((k0, ks, XOFF + dlt))
                    k0 += 128
                groups.append(dict(q0=t0, n=n, qtok=ti, blocks=blocks))
            ksrc = lambda hh, k0, ks: (bs.k[64 * hh:64 * hh + 64, k0:k0 + ks], [('k', bid, k0 // 512)] + bs.xt)
            vsrc = lambda hh, k0, ks: (bs.v[0:ks, k0 // 128, hh * 128:(hh + 1) * 128],
                                       [('v', bid, (k0 // 128) // 4)] + bs.xt)

            def attn(hook):
                ob = s.otr.get()
                gmb = {}
                for hh in range(2):
                    gb = s.gmr.get()
                    gmb[hh] = gb
                    s.dma('pool', s.gm[:, gb, :], s.gm_d[lb, 2 * j + hh], [], [('gm', gb)])
                s.sm_attn(groups, ob, bs, ksrc, vsrc, gmb, hook=hook)
                return ob
            attn.nsteps = sum(len(g_['blocks']) for g_ in groups) + 2
        return chunks, attn

    def mixer(s, tiles, l, sq, sample):
        c = s.cfg
        isA = l < c.LA
        s.norm(tiles, c.gcol('mix', l))
        s.load_mem_kv(l, sample)
        aug = 0.0 if isA else 1.0
        for bs in s.bsets:
            s.set_aug(bs, aug)
        rot_ov = s.ovr_a if isA else s.mr
        chunks, attn = s.pair_prep(tiles, l, 0, sq, sample, s.bsets[1], s.mr)
        for ch in chunks:
            ch()
        for j in range(c.NPA):
            nxt = None
            if j + 1 < c.NPA:
                nchunks, nattn = s.pair_prep(tiles, l, j + 1, sq, sample, s.bsets[j % 2], rot_ov)
                pending = list(nchunks)
                ndma = 2 if (sample or not isA) else 1
                if isA:
                    stride = max(2, (attn.nsteps - 4) // max(1, len(pending)))

                    def hook(st, pending=pending, stride=stride):
                        if st % stride == 0 and pending:
                            pending.pop(0)()
                else:
                    def hook(st, pending=pending, left=[ndma]):
                        if left[0] > 0 and pending:
                            left[0] -= 1
                            pending.pop(0)()
            else:
                pending = []
                hook = None
            ob = attn(hook)
            while pending:
                pending.pop(0)()
            s.out_proj(tiles, l, j, ob)
            if j + 1 < c.NPA:
                attn = nattn
        for jm in range(c.NPM):
            wkey = ('inam', l, jm) if isA else ('inbm', l - c.LA, jm)
            s.mem_pair(tiles, l, jm, wkey, sample)

    def kvb(s, tiles, sq, sample):
        c = s.cfg
        T = c.TS if sample else c.T
        si = c.NSEQ if sample else sq
        s.norm(tiles, c.gcol('kv'))
        s.set_aug(s.bsets[0], 1.0)
        keep0 = 0 if sample else T - c.KEEP
        for j in range(c.NPA):
            wb = s.winr.get()
            s.load_w(('kvb', j), s.win[:, wb, 0:c.NCH * 2 * 128], ('win', wb))

            def evk(ti, t0, n, pb, j=j):
                f = s.fpr.get()
                s.cp('dve', s.fp[:, f, 0:n], s.ps[pb][:, 0:n], [('ps', pb)], [('fp', f)])
                if sample:
                    s.dma('sp', s.bkT_s[j * 128:(j + 1) * 128, t0:t0 + n], s.fp[:, f, 0:n], [('fp', f)], [])
                elif t0 + n > keep0:
                    a = max(t0, keep0)
                    s.dma('sp', s.bkT_p[sq, j * 128:(j + 1) * 128, a - keep0:t0 + n - keep0], s.fp[:, f, a - t0:n],
                          [('fp', f)], [])
                s.dma('pool', s.kb_sc[si, j, :, t0:t0 + n], s.fp[:, f, 0:n], [('fp', f)], [('kbsc', si, j)])

            s.proj_fm(tiles, wb, lambda ch: (ch * 2 + 0) * 128, evk)

            def evv(u0, nu, rows, pb, j=j):
                f = s.fpr.get()
                s.cp('dve', s.fp[0:rows, f, 0:nu * 128], s.ps[pb][0:rows, 0:nu * 128], [('ps', pb)], [('fp', f)])
                src = s.fp[0:rows, f, 0:nu * 128].rearrange("p (u d) -> p u d", d=128)
                if sample:
                    s.dma('sp', s.bv_s[j, 0:rows, :].rearrange("(u p) d -> p u d", p=rows), src, [('fp', f)], [])
                else:
                    for uu in range(nu):
                        tk0 = (u0 + uu) * 128
                        if tk0 >= keep0:
                            s.dma('sp', s.bv_p[sq, j, tk0 - keep0:tk0 - keep0 + 128, :], s.fp[:, f, uu * 128:(uu + 1) * 128],
                                  [('fp', f)], [])
                vt = [('v', 0, kt // 4) for kt in sorted(set([u0, u0 + nu - 1]))]
                s.cp('pool', s.vb[0:rows, u0:u0 + nu, 0:64], src[:, :, 0:64], [('fp', f)], vt)
                s.cp('pool', s.vb[0:rows, u0:u0 + nu, 192:256], src[:, :, 64:128], [('fp', f)], vt)
                if sample:
                    s.dma('sp', s.vb_sc[si, j, 0:rows, :], s.vb[0:rows, 0, :], vt, [('vbsc', si, j)])
                else:
                    s.dma('sp', s.vb_sc[si, j, u0 * 128:(u0 + nu) * 128, :].rearrange("(u p) f -> p u f", p=128),
                          s.vb[:, u0:u0 + nu, :], vt, [('vbsc', si, j)])

            s.proj_tm(T, wb, lambda ch: (ch * 2 + 1) * 128, evv)

    def run_seq(s, sq, sample):
        c = s.cfg
        T = c.TS if sample else c.T
        tiles = [(i, t0, min(512, T - t0)) for i, t0 in enumerate(range(0, T, 512))]
        if not sample:
            s.mem_prep(sq)
        src = s.xT_s if sample else s.xT_p[sq]
        xt = [('x', ch, ti) for ch in range(c.NCH) for (ti, _, _) in tiles]
        s.dma('sp', s.xT[:, :, 0:T], src.rearrange("(c p) t -> p c t", p=128), [], xt)
        for l in range(c.L):
            s.ffn(tiles, l, 1)
            s.mixer(tiles, l, sq, sample)
            s.ffn(tiles, l, 2)
            if l == c.LA - 1:
                s.kvb(tiles, sq, sample)
        gc = c.gcol('final')
        for tile in tiles:
            (ti, t0, n) = tile
            r = s.norm_stats(tile)
            for ch in range(c.NCH):
                f = s.fpr.get()
                s.stt('dve', s.fp[:, f, 0:n], s.xT[:, ch, t0:t0 + n], s.gains[:, gc + ch:gc + ch + 1],
                      s.rs[:, r, 0:n], ALU.mult, ALU.mult, [('x', ch, ti), ('rs', r), ('gains',)], [('fp', f)])
                dst = (s.yT_s[ch * 128:(ch + 1) * 128, t0:t0 + n] if sample
                       else s.yT_p[sq, ch * 128:(ch + 1) * 128, t0:t0 + n])
                s.dma('sp', dst, s.fp[:, f, 0:n], [('fp', f)], [])

    def build(s):
        s.prologue()
        for sq in range(s.cfg.NSEQ):
            s.run_seq(sq, False)
        s.run_seq(0, True)
        s.stats = s.P.emit(s.nc)
        return s.nc


def make_in_maps(cfg, inp):
    c = cfg
    nc_ = c.n_cores
    f = lambda a: np.ascontiguousarray(a, dtype=np.float32)
    wpack = pack_weights(c, inp)
    gains = pack_gains(c, inp)
    consts = make_consts(c)
    gm = make_gm(c, np.asarray(inp['rel_bias_b'], np.float32))
    xp = np.asarray(inp['x_prompt']).reshape(nc_, c.NSEQ, c.T, c.D)
    mp = np.asarray(inp['mem_prompt']).reshape(nc_, c.NSEQ, c.NMEM, c.D)
    maps = []
    for k in range(nc_):
        m = {
            'xT_p': f(xp[k].transpose(0, 2, 1)),
            'memT_p': f(mp[k].transpose(0, 2, 1)),
            'xT_s': f(np.asarray(inp['x_sample'])[k].T),
            'cakT': f(np.asarray(inp['cache_a_k'])[:, k].reshape(c.LA, c.PAST, c.AW).transpose(0, 2, 1)),
            'cav': f(np.asarray(inp['cache_a_v'])[:, k].reshape(c.LA, c.PAST, c.AW)),
            'cbkT': f(np.asarray(inp['cache_b_k'])[k].reshape(c.WBC, c.AW).T),
            'cbv': f(np.asarray(inp['cache_b_v'])[k].reshape(c.WBC, c.AW)),
            'cmkT': f(np.asarray(inp['cache_mem_k'])[:, k].reshape(c.L, c.NMEM, c.MW).transpose(0, 2, 1)),
            'cmv': f(np.asarray(inp['cache_mem_v'])[:, k].reshape(c.L, c.NMEM, c.MW)),
            'gains': gains, 'consts': consts, 'gm': gm, 'wpack': wpack,
        }
        maps.append(m)
    return maps


def assemble(cfg, results):
    c = cfg
    cat = lambda name: np.stack([np.asarray(r[name]) for r in results])
    B = c.n_cores * c.NSEQ
    y_prompt = cat('yT_p').transpose(0, 1, 3, 2).reshape(B, c.T, c.D)
    y_sample = cat('yT_s').transpose(0, 2, 1)
    ak = cat('akT_p').transpose(1, 0, 2, 4, 3).reshape(c.LA, B, c.T, c.HA, 64)
    av = cat('av_p').transpose(1, 0, 2, 4, 3, 5).reshape(c.LA, B, c.T, c.HA, 64)
    bk = cat('bkT_p').transpose(0, 1, 3, 2).reshape(B, c.KEEP, c.HA, 64)
    bv = cat('bv_p').transpose(0, 1, 3, 2, 4).reshape(B, c.KEEP, c.HA, 64)
    mk = cat('mkT_p').transpose(1, 0, 2, 4, 3).reshape(c.L, B, c.NMEM, c.HM, 64)
    mv = cat('mv_p').transpose(1, 0, 2, 4, 3, 5).reshape(c.L, B, c.NMEM, c.HM, 64)
    aks = cat('akT_s').transpose(1, 0, 3, 2).reshape(c.LA, c.n_cores, c.TS, c.HA, 64)
    avs = cat('av_s').transpose(1, 0, 3, 2, 4).reshape(c.LA, c.n_cores, c.TS, c.HA, 64)
    bks = cat('bkT_s').transpose(0, 2, 1).reshape(c.n_cores, c.TS, c.HA, 64)
    bvs = cat('bv_s').transpose(0, 2, 1, 3).reshape(c.n_cores, c.TS, c.HA, 64)
    outs = (y_prompt, y_sample, ak, av, bk, bv, mk, mv, aks, avs, bks, bvs)
    return tuple(np.ascontiguousarray(o, dtype=np.float32) for o in outs)


def run(cfg, inp, trace=False):
    b = Builder(cfg)
    nc = b.build()
    maps = make_in_maps(cfg, inp)
    res = run_bass_kernel_spmd(nc, maps, core_ids=list(range(cfg.n_cores)), trace=trace)
    return assemble(cfg, res.results), res, b


def kernel(**inputs):
    cfg = Cfg()
    outs, _, _ = run(cfg, inputs)
    return outs
```
